# Optimizing a Trainium2 kernel written in Bass

```python
import math
import jax, jax.numpy as jnp
from jax import lax
import numpy as np

D_MODEL = 1024
BATCH = 32
SEQ = 2048
DEPTH = 2
DEC_BATCH = 8
DEC_SEQ = 16
PAST_LEN = 2048

CHUNK = 64
Q_BLOCK = 128
EPS = 1e-6
NEG = -1e30
F_MIN = 1e-12
MLA_HEADS = 8
MLA_NOPE = 64
MLA_ROPE = 32
MLA_V = 64
MLA_Q_LORA = 384
MLA_KV_LORA = 256
ROPE_THETA = 10000.0
MLA_SCALE = (MLA_NOPE + MLA_ROPE) ** -0.5
HG_HEADS = 8
HG_DK = 64
HG_DV = 64
HG_BLOCK = 16
DF_HEADS = 8
DF_DH = 32
DF_DV = 2 * DF_DH
DF_SCALE = DF_DH ** -0.5
N_BRANCH = 3
BR_WIDTH = 512
D_FF = 2816
CONV_W = 3
IN_SIZES = (MLA_Q_LORA, MLA_KV_LORA, MLA_ROPE,
            HG_HEADS * HG_DK, HG_HEADS * HG_DK, HG_HEADS * HG_DV, HG_HEADS * HG_DV,
            DF_HEADS * 2 * DF_DH, DF_HEADS * 2 * DF_DH, DF_HEADS * DF_DV,
            N_BRANCH * D_MODEL)
IN_COLS = sum(IN_SIZES)

kernel_name = 'hybrid_stream_mla_hgrn2_diffattn'


def _rms(x, g):
    xf = x.astype(jnp.float32)
    r = lax.rsqrt(jnp.mean(xf * xf, axis=-1, keepdims=True) + EPS)
    return (xf * r).astype(x.dtype) * g


def _split_cols(z):
    out, start = [], 0
    for n in IN_SIZES:
        out.append(z[..., start:start + n])
        start += n
    return out


def _rope(x, pos):
    half = MLA_ROPE // 2
    inv = ROPE_THETA ** (-jnp.arange(half, dtype=jnp.float32) / half)
    ang = pos.astype(jnp.float32)[:, None] * inv[None, :]
    ang = ang.reshape((1, pos.shape[0]) + (1,) * (x.ndim - 3) + (half,))
    cos, sin = jnp.cos(ang), jnp.sin(ang)
    x1 = x[..., :half].astype(jnp.float32)
    x2 = x[..., half:].astype(jnp.float32)
    return jnp.concatenate([x1 * cos - x2 * sin, x2 * cos + x1 * sin], axis=-1).astype(x.dtype)


def _chunk_mask(q_pos, k_pos):
    return (k_pos[None, :] // CHUNK) <= (q_pos[:, None] // CHUNK)


def _over_query_blocks(fn, q_pos, qs):
    T = q_pos.shape[0]
    if T > Q_BLOCK and T % Q_BLOCK == 0:
        nb = T // Q_BLOCK
        pos_b = q_pos.reshape(nb, Q_BLOCK)
        qs_b = tuple(jnp.moveaxis(q.reshape((q.shape[0], nb, Q_BLOCK) + q.shape[2:]), 1, 0) for q in qs)
        out = lax.map(lambda a: fn(a[0], *a[1]), (pos_b, qs_b))
        out = jnp.moveaxis(out, 0, 1)
        return out.reshape((out.shape[0], T) + out.shape[3:])
    return fn(q_pos, *qs)


def _hgrn2(q, k, v, logf, S0):
    B, T, H, DK = q.shape
    DV = v.shape[-1]
    n = -(-T // HG_BLOCK)
    pad = n * HG_BLOCK - T

    def prep(a):
        a = jnp.pad(a.astype(jnp.float32), ((0, 0), (0, pad), (0, 0), (0, 0)))
        return jnp.moveaxis(a.reshape(B, n, HG_BLOCK, H, a.shape[-1]), 1, 0)

    tril = jnp.tril(jnp.ones((HG_BLOCK, HG_BLOCK), dtype=bool))[None, :, :, None, None]

    def step(S, blk):
        qb, kb, vb, gb = blk
        b = jnp.cumsum(gb, axis=1)
        o_inter = jnp.einsum('blhk,bhkv->blhv', qb * jnp.exp(b), S)
        d = b[:, :, None] - b[:, None, :]
        dec = jnp.where(tril, jnp.exp(jnp.where(tril, d, 0.0)), 0.0)
        att = jnp.einsum('bthk,btshk,bshk->bhts', qb, dec, kb)
        o_intra = jnp.einsum('bhts,bshv->bthv', att, vb)
        b_last = b[:, -1]
        S = jnp.exp(b_last)[..., None] * S + jnp.einsum(
            'bshk,bshv->bhkv', kb * jnp.exp(b_last[:, None] - b), vb)
        return S, o_inter + o_intra

    S, o = lax.scan(step, S0.astype(jnp.float32), (prep(q), prep(k), prep(v), prep(logf)))
    o = jnp.moveaxis(o, 0, 1).reshape(B, n * HG_BLOCK, H, DV)[:, :T]
    return o.astype(v.dtype), S


def _layer(l, x, pos, past, S0, conv0, lb, prm):
    (norm_mix_g, w_in, mla_q_norm_g, mla_w_uq, mla_kv_norm_g, mla_w_ukv, hgrn_norm_g,
     diff_lambda, diff_norm_g, w_branch, w_out, norm_ffn_g, ffn_w_up, ffn_conv_w,
     ffn_conv_b, ffn_w_down) = prm
    B, T, _ = x.shape
    h = _rms(x, norm_mix_g)
    z = h @ w_in
    q_lat, kv_lat, k_rot, hq, hf, hi, hg, dq, dk, dv, gate = _split_cols(z)

    q = (_rms(q_lat, mla_q_norm_g) @ mla_w_uq).reshape(B, T, MLA_HEADS, MLA_NOPE + MLA_ROPE)
    q_nope = q[..., :MLA_NOPE]
    q_rot = _rope(q[..., MLA_NOPE:], pos)
    ckv_new = _rms(kv_lat, mla_kv_norm_g)
    krot_new = _rope(k_rot, pos)
    if past is None:
        key_pos = pos
        ckv_all, krot_all = ckv_new, krot_new
        dk_all_flat = dk.reshape(B, T, DF_HEADS, 2 * DF_DH)
        dv_all = dv.reshape(B, T, DF_HEADS, DF_DV)
    else:
        key_pos = jnp.arange(past[0].shape[1] + T)
        ckv_all = jnp.concatenate([past[0], ckv_new], axis=1)
        krot_all = jnp.concatenate([past[1], krot_new], axis=1)
        dk_all_flat = jnp.concatenate([past[2], dk.reshape(B, T, DF_HEADS, 2 * DF_DH)], axis=1)
        dv_all = jnp.concatenate([past[3], dv.reshape(B, T, DF_HEADS, DF_DV)], axis=1)
    K = key_pos.shape[0]
    kv = (ckv_all @ mla_w_ukv).reshape(B, K, MLA_HEADS, MLA_NOPE + MLA_V)
    k_nope, v_mla = kv[..., :MLA_NOPE], kv[..., MLA_NOPE:]

    def mla_fn(qp, qn, qr):
        s = (jnp.einsum('bqhd,bkhd->bhqk', qn, k_nope)
             + jnp.einsum('bqhr,bkr->bhqk', qr, krot_all)).astype(jnp.float32) * MLA_SCALE
        s = jnp.where(_chunk_mask(qp, key_pos), s, NEG)
        p = jax.nn.softmax(s, axis=-1).astype(v_mla.dtype)
        return jnp.einsum('bhqk,bkhd->bqhd', p, v_mla)

    o_a = _over_query_blocks(mla_fn, pos, (q_nope, q_rot)).reshape(B, T, BR_WIDTH)

    fpre = hf.astype(jnp.float32).reshape(B, T, HG_HEADS, HG_DK)
    lbh = lb.reshape(HG_HEADS, HG_DK)
    sig_neg = jax.nn.sigmoid(-fpre)
    f_h = jax.nn.sigmoid(fpre) + lbh * sig_neg
    logf = jnp.log(jnp.maximum(f_h, F_MIN))
    k_h = (1.0 - lbh) * sig_neg
    q_h = jax.nn.silu(hq).reshape(B, T, HG_HEADS, HG_DK)
    v_h = hi.reshape(B, T, HG_HEADS, HG_DV)
    o_h, S_new = _hgrn2(q_h, k_h, v_h, logf, S0)
    o_b = (_rms(o_h, hgrn_norm_g) * jax.nn.silu(hg).reshape(B, T, HG_HEADS, HG_DV)).reshape(B, T, BR_WIDTH)

    lam_init = 0.8 - 0.6 * math.exp(-0.3 * l)
    lamf = diff_lambda.astype(jnp.float32)
    lam = jnp.exp(jnp.sum(lamf[0] * lamf[1])) - jnp.exp(jnp.sum(lamf[2] * lamf[3])) + lam_init
    dq_h = dq.reshape(B, T, DF_HEADS, 2, DF_DH)
    dk_all = dk_all_flat.reshape(B, K, DF_HEADS, 2, DF_DH)

    def diff_fn(qp, qd):
        s = jnp.einsum('bqhjd,bkhjd->bjhqk', qd, dk_all).astype(jnp.float32) * DF_SCALE
        s = jnp.where(_chunk_mask(qp, key_pos), s, NEG)
        p = jax.nn.softmax(s, axis=-1)
        a = (p[:, 0] - lam * p[:, 1]).astype(dv_all.dtype)
        return jnp.einsum('bhqk,bkhd->bqhd', a, dv_all)

    o_c = _over_query_blocks(diff_fn, pos, (dq_h,))
    o_c = (_rms(o_c, diff_norm_g) * (1.0 - lam_init)).reshape(B, T, BR_WIDTH)

    g = jax.nn.sigmoid(gate.astype(jnp.float32)).astype(x.dtype).reshape(B, T, N_BRANCH, D_MODEL)
    br = jnp.stack([o_a, o_b, o_c], axis=2)
    proj = jnp.einsum('btnw,nwd->btnd', br, w_branch)
    x = x + jnp.sum(g * proj, axis=2) @ w_out

    u = _rms(x, norm_ffn_g) @ ffn_w_up
    a_up, v_up = u[..., :D_FF], u[..., D_FF:]
    ext = jnp.concatenate([conv0.astype(a_up.dtype), a_up], axis=1)
    c = ffn_conv_b + sum(ext[:, j:j + T] * ffn_conv_w[j] for j in range(CONV_W))
    x = x + (jax.nn.silu(c) * v_up) @ ffn_w_down
    conv_new = ext[:, T:]

    return x, (ckv_new, krot_new, dk.reshape(B, T, DF_HEADS, 2 * DF_DH),
               dv.reshape(B, T, DF_HEADS, DF_DV), S_new, conv_new)


def _nrm(k, shape, scale):
    return jax.random.normal(k, shape, jnp.float32) * scale


def setup_inputs(seed: int = 0) -> dict:
    key = jax.random.key(seed)
    ks = jax.random.split(key, 26)
    return {
        'x_prompt': _nrm(ks[0], (BATCH, SEQ, D_MODEL), 1.0),
        'x_sample': _nrm(ks[1], (DEC_BATCH, DEC_SEQ, D_MODEL), 1.0),
        'cache_mla_ckv': _nrm(ks[2], (DEPTH, DEC_BATCH, PAST_LEN, MLA_KV_LORA), 1.0),
        'cache_mla_krope': _nrm(ks[3], (DEPTH, DEC_BATCH, PAST_LEN, MLA_ROPE), 1.0),
        'cache_diff_k': _nrm(ks[4], (DEPTH, DEC_BATCH, PAST_LEN, DF_HEADS, 2 * DF_DH), 1.0),
        'cache_diff_v': _nrm(ks[5], (DEPTH, DEC_BATCH, PAST_LEN, DF_HEADS, DF_DV), 1.0),
        'state_hgrn': _nrm(ks[6], (DEPTH, DEC_BATCH, HG_HEADS, HG_DK, HG_DV), 0.5),
        'state_ffn_conv': _nrm(ks[7], (DEPTH, DEC_BATCH, CONV_W - 1, D_FF), 1.0),
        'norm_mix_g': 1.0 + _nrm(ks[8], (DEPTH, D_MODEL), 0.01),
        'w_in': _nrm(ks[9], (DEPTH, D_MODEL, IN_COLS), D_MODEL ** -0.5),
        'mla_q_norm_g': 1.0 + _nrm(ks[10], (DEPTH, MLA_Q_LORA), 0.01),
        'mla_w_uq': _nrm(ks[11], (DEPTH, MLA_Q_LORA, MLA_HEADS * (MLA_NOPE + MLA_ROPE)), MLA_Q_LORA ** -0.5),
        'mla_kv_norm_g': 1.0 + _nrm(ks[12], (DEPTH, MLA_KV_LORA), 0.01),
        'mla_w_ukv': _nrm(ks[13], (DEPTH, MLA_KV_LORA, MLA_HEADS * (MLA_NOPE + MLA_V)), MLA_KV_LORA ** -0.5),
        'hgrn_lb_logits': _nrm(ks[14], (DEPTH, HG_HEADS * HG_DK), 1.0),
        'hgrn_norm_g': 1.0 + _nrm(ks[15], (DEPTH, HG_DV), 0.01),
        'diff_lambda': _nrm(ks[16], (DEPTH, 4, DF_DH), 0.1),
        'diff_norm_g': 1.0 + _nrm(ks[17], (DEPTH, DF_DV), 0.01),
        'w_branch': _nrm(ks[18], (DEPTH, N_BRANCH, BR_WIDTH, D_MODEL), BR_WIDTH ** -0.5),
        'w_out': _nrm(ks[19], (DEPTH, D_MODEL, D_MODEL), D_MODEL ** -0.5),
        'norm_ffn_g': 1.0 + _nrm(ks[20], (DEPTH, D_MODEL), 0.01),
        'ffn_w_up': _nrm(ks[21], (DEPTH, D_MODEL, 2 * D_FF), D_MODEL ** -0.5),
        'ffn_conv_w': _nrm(ks[22], (DEPTH, CONV_W, D_FF), CONV_W ** -0.5),
        'ffn_conv_b': _nrm(ks[23], (DEPTH, D_FF), 0.01),
        'ffn_w_down': _nrm(ks[24], (DEPTH, D_FF, D_MODEL), D_FF ** -0.5),
        'norm_final_g': 1.0 + _nrm(ks[25], (D_MODEL,), 0.01),
    }


def _stack(states, i):
    return jnp.stack([s[i] for s in states], axis=0)


def reference(x_prompt, x_sample, cache_mla_ckv, cache_mla_krope, cache_diff_k, cache_diff_v,
              state_hgrn, state_ffn_conv, norm_mix_g, w_in, mla_q_norm_g, mla_w_uq, mla_kv_norm_g,
              mla_w_ukv, hgrn_lb_logits, hgrn_norm_g, diff_lambda, diff_norm_g, w_branch, w_out,
              norm_ffn_g, ffn_w_up, ffn_conv_w, ffn_conv_b, ffn_w_down, norm_final_g):
    B, T = x_prompt.shape[0], x_prompt.shape[1]
    Ts = x_sample.shape[1]
    P = cache_mla_ckv.shape[2]
    pos_p = jnp.arange(T)
    pos_s = P + jnp.arange(Ts)
    lb_soft = jax.nn.softmax(hgrn_lb_logits.astype(jnp.float32), axis=0)
    lb_all = jnp.cumsum(lb_soft, axis=0) - lb_soft[0]
    yp, ys = x_prompt, x_sample
    sp, ss = [], []
    for l in range(DEPTH):
        prm = (norm_mix_g[l], w_in[l], mla_q_norm_g[l], mla_w_uq[l], mla_kv_norm_g[l], mla_w_ukv[l],
               hgrn_norm_g[l], diff_lambda[l], diff_norm_g[l], w_branch[l], w_out[l], norm_ffn_g[l],
               ffn_w_up[l], ffn_conv_w[l], ffn_conv_b[l], ffn_w_down[l])
        yp, st_p = _layer(l, yp, pos_p, None,
                          jnp.zeros((B, HG_HEADS, HG_DK, HG_DV), jnp.float32),
                          jnp.zeros((B, CONV_W - 1, D_FF), x_prompt.dtype), lb_all[l], prm)
        sp.append(st_p)
        ys, st_s = _layer(l, ys, pos_s,
                          (cache_mla_ckv[l], cache_mla_krope[l], cache_diff_k[l], cache_diff_v[l]),
                          state_hgrn[l], state_ffn_conv[l], lb_all[l], prm)
        ss.append(st_s)
    y_prompt = _rms(yp, norm_final_g)
    y_sample = _rms(ys, norm_final_g)
    p_ckv, p_krope, p_dk, p_dv = _stack(sp, 0), _stack(sp, 1), _stack(sp, 2), _stack(sp, 3)
    p_hgrn, p_conv = _stack(sp, 4), _stack(sp, 5)
    s_ckv, s_krope, s_dk, s_dv = _stack(ss, 0), _stack(ss, 1), _stack(ss, 2), _stack(ss, 3)
    s_hgrn, s_conv = _stack(ss, 4), _stack(ss, 5)
    return (y_prompt, y_sample, p_ckv, p_krope, p_dk, p_dv, p_hgrn, p_conv,
            s_ckv, s_krope, s_dk, s_dv, s_hgrn, s_conv)
```

```python
import contextlib
import math

import ml_dtypes
import numpy as np

import concourse.bass as bass
import concourse.mybir as mybir
from concourse.bass_utils import run_bass_kernel_spmd

F32 = mybir.dt.float32
BF16 = mybir.dt.bfloat16
AF = mybir.ActivationFunctionType
ALU = mybir.AluOpType
AX = mybir.AxisListType

D = 1024
CHUNK = 64
EPS = 1e-6
F_MIN = 1e-12
MH, MN, MR, MV = 8, 64, 32, 64
QL, KVL = 384, 256
THETA = 10000.0
MLA_SCALE = (MN + MR) ** -0.5
HH, HDK, HDV = 8, 64, 64
DH, DDH, DDV = 8, 32, 64
DF_SCALE = DDH ** -0.5
DFF = 2816
NFC = DFF // 128
IN_COLS = 7328
OFF_Q, OFF_KV, OFF_KR, OFF_HQ, OFF_HF, OFF_HI, OFF_HG, OFF_DQ, OFF_DK, OFF_DV, OFF_G = (
    0, 384, 640, 672, 1184, 1696, 2208, 2720, 3232, 3744, 4256)

ENGS = ("pe", "act", "dve", "pool", "sp")
N_DMA_SEMS = 10


class Res:
    __slots__ = ("name", "writer", "readers", "excl")

    def __init__(self, name, excl=False):
        self.name = name
        self.writer = None
        self.readers = []
        self.excl = excl


class Op:
    __slots__ = ("eng", "fn", "deps", "signal", "sem", "count", "is_dma")

    def __init__(self, eng, fn, is_dma):
        self.eng = eng
        self.fn = fn
        self.deps = []
        self.signal = False
        self.sem = None
        self.count = 0
        self.is_dma = is_dma


class Prog:
    def __init__(self, nc):
        self.nc = nc
        self.ops = {e: [] for e in ENGS}
        self.dma_rr = {e: 0 for e in ENGS}
        self.dma_last = {}

    def op(self, eng, fn, reads=(), writes=(), dma=False):
        o = Op(eng, fn, dma)
        deps = {}
        if any(r.excl for r in reads):
            writes = list(writes) + [r for r in reads if r.excl]
            reads = [r for r in reads if not r.excl]
        for r in reads:
            if r.writer is not None:
                deps[id(r.writer)] = r.writer
        for w in writes:
            if w.writer is not None:
                deps[id(w.writer)] = w.writer
            for rd in w.readers:
                deps[id(rd)] = rd
        if dma:
            slot = self.dma_rr[eng] % N_DMA_SEMS
            self.dma_rr[eng] += 1
            prev = self.dma_last.get((eng, slot))
            if prev is not None:
                deps[id(prev)] = prev
            self.dma_last[(eng, slot)] = o
            o.sem = (eng, slot)
            o.signal = True
        for d in deps.values():
            if d is o:
                continue
            if d.eng == "pe" and eng == "pe" and not d.is_dma and not dma:
                continue
            o.deps.append(d)
            d.signal = True
        for w in writes:
            w.writer = o
            w.readers = []
        for r in reads:
            r.readers.append(o)
        self.ops[eng].append(o)
        return o

    def emit(self, final_ops):
        nc = self.nc
        for o in final_ops:
            o.signal = True
        with contextlib.ExitStack() as st:
            sems = {}
            for e in ENGS:
                sems[e] = st.enter_context(nc.semaphore("s_" + e))
            for e in ("sp", "pool"):
                for k in range(N_DMA_SEMS):
                    sems[(e, k)] = st.enter_context(nc.semaphore("d_%s_%d" % (e, k)))
            cnt = {}
            for e in ENGS:
                for o in self.ops[e]:
                    if not o.signal:
                        continue
                    key = o.sem if o.is_dma else e
                    o.sem = key
                    cnt[key] = cnt.get(key, 0) + (16 if o.is_dma else 1)
                    o.count = cnt[key]
            block = st.enter_context(nc.Block())

            def run(e, eng):
                waited = {}
                for o in self.ops[e]:
                    for d in o.deps:
                        if waited.get(d.sem, 0) < d.count:
                            eng.wait_ge(sems[d.sem], d.count)
                            waited[d.sem] = d.count
                    ins = o.fn(eng)
                    if o.signal:
                        ins.then_inc(sems[o.sem], 16 if o.is_dma else 1)
                if e == "sp":
                    for key, total in cnt.items():
                        if isinstance(key, tuple) and waited.get(key, 0) < total:
                            eng.wait_ge(sems[key], total)
                            waited[key] = total

            @block.tensor
            def _(eng):
                run("pe", eng)

            @block.scalar
            def _(eng):
                run("act", eng)

            @block.vector
            def _(eng):
                run("dve", eng)

            @block.gpsimd
            def _(eng):
                run("pool", eng)

            @block.sync
            def _(eng):
                run("sp", eng)


class Ring:
    def __init__(self, tiles, name, excl=False):
        self.tiles = tiles
        self.res = [Res("%s%d" % (name, i), excl) for i in range(len(tiles))]
        self.i = 0

    def next(self):
        k = self.i % len(self.tiles)
        self.i += 1
        return self.tiles[k], self.res[k]


def _consts(npt):
    s = np.arange(128)[:, None]
    t = np.arange(128)[None, :]
    tri = (s <= t).astype(np.float32)
    tribd = ((s <= t) & (s // 64 == t // 64)).astype(np.float32)
    dc = ((s > t) & (s < 64)).astype(np.float32)[:, :64]
    dmat = (s > t).astype(np.float32)
    hgm = np.concatenate([tribd, -tribd, tri, dc, dmat], axis=1)
    half = MR // 2
    inv = THETA ** (-np.arange(half, dtype=np.float32) / half)
    pos = (np.arange(npt)[None, :] * 128 + np.arange(128)[:, None]).astype(np.float32)
    ang = pos[:, :, None] * inv[None, None, :]
    m01 = np.zeros((128, 4), np.float32)
    for v in range(4):
        m01[:, v] = (np.arange(128) // 32 == v)
    return {
        "c_identb": np.eye(128).astype(ml_dtypes.bfloat16),
        "c_identf": np.eye(128).astype(np.float32),
        "c_hgm": hgm.astype(ml_dtypes.bfloat16),
        "c_maskbd": tribd.astype(ml_dtypes.bfloat16),
        "c_cos": np.cos(ang).astype(np.float32),
        "c_sin": np.sin(ang).astype(np.float32),
        "c_m01": m01,
    }


def build(NP=4, T=2048, DEPTH=2, PAST=2048, TS=16, SAMPLE=True, dbg=()):
    nc = bass.Bass("TRN2", target_bir_lowering=False)
    P = Prog(nc)
    NBLK = T // 512
    KT_P = T // 128
    PT_S = PAST // 128
    KT_S = PT_S + 1
    KTMAX = max(KT_P, KT_S if SAMPLE else 0)
    KMAX = max(T, PAST + TS if SAMPLE else 0)
    NPT = KTMAX
    NSEQ = NP + (1 if SAMPLE else 0)

    def din(name, shape, dt=F32):
        return nc.dram_tensor(name, list(shape), dt, kind="ExternalInput").ap()

    def dout(name, shape):
        return nc.dram_tensor(name, list(shape), F32, kind="ExternalOutput").ap()

    def dscr(name, shape, dt):
        return nc.dram_tensor(name, list(shape), dt).ap()

    x_prompt = din("x_prompt", [NP, T, D])
    if SAMPLE:
        x_sample = din("x_sample", [1, TS, D])
        c_ckv = din("cache_mla_ckv", [DEPTH, 1, PAST, KVL])
        c_kr = din("cache_mla_krope", [DEPTH, 1, PAST, MR])
        c_dk = din("cache_diff_k", [DEPTH, 1, PAST, DH * 2 * DDH])
        c_dv = din("cache_diff_v", [DEPTH, 1, PAST, DH * DDV])
        st_hg = din("state_hgrn", [DEPTH, 1, HH, HDK, HDV])
        st_cv = din("state_ffn_conv", [DEPTH, 1, 2, DFF])
    norm_mix_g = din("norm_mix_g", [DEPTH, D])
    w_in = din("w_in", [DEPTH, D, IN_COLS])
    mla_q_norm_g = din("mla_q_norm_g", [DEPTH, QL])
    mla_w_uq = din("mla_w_uq", [DEPTH, QL, MH * (MN + MR)])
    mla_kv_norm_g = din("mla_kv_norm_g", [DEPTH, KVL])
    mla_w_ukv = din("mla_w_ukv", [DEPTH, KVL, MH * (MN + MV)])
    hgrn_lb_logits = din("hgrn_lb_logits", [DEPTH, HH * HDK])
    hgrn_norm_g = din("hgrn_norm_g", [DEPTH, HDV])
    diff_lambda = din("diff_lambda", [DEPTH, 4, DDH])
    diff_norm_g = din("diff_norm_g", [DEPTH, DDV])
    w_branch = din("w_branch", [DEPTH, 3, 512, D])
    w_out = din("w_out", [DEPTH, D, D])
    norm_ffn_g = din("norm_ffn_g", [DEPTH, D])
    ffn_w_up = din("ffn_w_up", [DEPTH, D, 2 * DFF])
    ffn_conv_w = din("ffn_conv_w", [DEPTH, 3, DFF])
    ffn_conv_b = din("ffn_conv_b", [DEPTH, DFF])
    ffn_w_down = din("ffn_w_down", [DEPTH, DFF, D])
    norm_final_g = din("norm_final_g", [D])
    c_identb = din("c_identb", [128, 128], BF16)
    c_identf = din("c_identf", [128, 128])
    c_hgm = din("c_hgm", [128, 576], BF16)
    c_maskbd = din("c_maskbd", [128, 128], BF16)
    c_cos = din("c_cos", [128, NPT, 16])
    c_sin = din("c_sin", [128, NPT, 16])
    c_m01 = din("c_m01", [128, 4])

    y_prompt = dout("y_prompt", [NP, T, D])
    p_ckv = dout("p_ckv", [DEPTH, NP, T, KVL])
    p_krope = dout("p_krope", [DEPTH, NP, T, MR])
    p_dk = dout("p_dk", [DEPTH, NP, T, 512])
    p_dv = dout("p_dv", [DEPTH, NP, T, 512])
    p_hgrn = dout("p_hgrn", [DEPTH, NP, HH, HDK, HDV])
    p_conv = dout("p_conv", [DEPTH, NP, 2, DFF])
    if SAMPLE:
        y_sample = dout("y_sample", [1, TS, D])
        s_ckv = dout("s_ckv", [DEPTH, 1, TS, KVL])
        s_krope = dout("s_krope", [DEPTH, 1, TS, MR])
        s_dk = dout("s_dk", [DEPTH, 1, TS, 512])
        s_dv = dout("s_dv", [DEPTH, 1, TS, 512])
        s_hgrn = dout("s_hgrn", [DEPTH, 1, HH, HDK, HDV])
        s_conv = dout("s_conv", [DEPTH, 1, 2, DFF])
    dbg_out = {}
    for name, shape in dbg:
        dbg_out[name] = dout("dbg_" + name, shape)

    xs = dscr("xs", [NSEQ, T, D], F32)
    bcd = dscr("bcd", [4, 512], F32)
    bcd_ring = Ring([bcd[i] for i in range(4)], "bcd")
    r_xs = [[Res("xs%d_%d" % (s, j)) for j in range(NBLK)] for s in range(NSEQ)]

    pieces = {}
    conv_list = []

    def piece(l, name, shape, srcs):
        scr = dscr("w%d_%s" % (l, name), [128] + list(shape), BF16)
        r = Res("w%d_%s" % (l, name))
        pieces[(l, name)] = (scr, shape, r)
        for dv, src in srcs:
            conv_list.append((dv(scr), src, r))

    for l in range(DEPTH):
        wi = w_in[l].rearrange("(kc p) n -> p kc n", p=128)

        def colpiece(name, c0, n, l=l, wi=wi):
            piece(l, name, [8, n], [(lambda s: s[:, :, :], wi[:, :, c0:c0 + n])])

        colpiece("q0", 0, 256)
        colpiece("q1", 256, 128)
        colpiece("kv", OFF_KV, 256)
        colpiece("kr", OFF_KR, 32)
        for gname, off in (("hq", OFF_HQ), ("hf", OFF_HF), ("hi", OFF_HI), ("hg", OFF_HG),
                           ("dq", OFF_DQ), ("dk", OFF_DK), ("dv", OFF_DV)):
            colpiece(gname + "0", off, 256)
            colpiece(gname + "1", off + 256, 256)
        wb = w_branch[l].rearrange("n (kc p) d -> p kc n d", p=128)
        for d in range(8):
            g0 = OFF_G + d * 128
            piece(l, "gA%d" % d, [8, 2, 128], [
                (lambda s: s[:, :, 0, :], wi[:, :, g0:g0 + 128]),
                (lambda s: s[:, :, 1, :], wi[:, :, g0 + 1024:g0 + 1024 + 128])])
            piece(l, "gB%d" % d, [8, 128], [(lambda s: s[:, :, :], wi[:, :, g0 + 2048:g0 + 2048 + 128])])
            piece(l, "br%d" % d, [4, 3, 128], [
                (lambda s, n=n: s[:, :, n, :], wb[:, :, n, d * 128:(d + 1) * 128]) for n in range(3)])
        uq = mla_w_uq[l].rearrange("(kc p) n -> p kc n", p=128)
        piece(l, "uq0", [3, 384], [(lambda s: s[:, :, :], uq[:, :, 0:384])])
        piece(l, "uq1", [3, 384], [(lambda s: s[:, :, :], uq[:, :, 384:768])])
        ukv = mla_w_ukv[l].rearrange("(kc p) (h c) -> p kc h c", p=128, c=128)
        piece(l, "ukk", [2, 8, 64], [(lambda s, c=c: s[:, c, :, :], ukv[:, c, :, 0:64]) for c in range(2)])
        piece(l, "ukv", [2, 8, 64], [(lambda s, c=c: s[:, c, :, :], ukv[:, c, :, 64:128]) for c in range(2)])
        wo = w_out[l].rearrange("(kc p) n -> p kc n", p=128)
        for i in range(4):
            piece(l, "wo%d" % i, [8, 256], [(lambda s: s[:, :, :], wo[:, :, i * 256:(i + 1) * 256])])
        wu = ffn_w_up[l].rearrange("(kc p) n -> p kc n", p=128)
        for c in range(NFC):
            piece(l, "up%d" % c, [8, 2, 128], [
                (lambda s: s[:, :, 0, :], wu[:, :, c * 128:(c + 1) * 128]),
                (lambda s: s[:, :, 1, :], wu[:, :, DFF + c * 128:DFF + (c + 1) * 128])])
        wd = ffn_w_down[l].rearrange("(c p) n -> p c n", p=128)
        for fh in range(2):
            for dh in range(2):
                for gi in range(3):
                    c0 = fh * 11 + gi * 4
                    ncn = min(4, fh * 11 + 11 - c0)
                    piece(l, "dn%d_%d_%d" % (fh, dh, gi), [ncn, 512], [
                        (lambda s: s[:, :, :], wd[:, c0:c0 + ncn, dh * 512:(dh + 1) * 512])])

    def block_piece_order():
        o = ["q0", "q1", "uq0", "uq1", "ukv", "ukk", "kv", "kr",
             "hq0", "hq1", "hf0", "hf1", "hi0", "hi1", "hg0", "hg1",
             "dq0", "dq1", "dk0", "dk1", "dv0", "dv1"]
        for d in range(8):
            o += ["gA%d" % d, "gB%d" % d, "br%d" % d]
        o += ["wo0", "wo1", "wo2", "wo3"]
        for fh in range(2):
            o += ["up%d" % c for c in range(fh * 11, fh * 11 + 11)]
            for dh in range(2):
                o += ["dn%d_%d_%d" % (fh, dh, gi) for gi in range(3)]
        return o

    past_piece_order = ["ukv", "ukk"]

    seqs = [dict(kind="p", idx=s, nblk=NBLK) for s in range(NP)]
    if SAMPLE:
        seqs.append(dict(kind="s", idx=NP, nblk=1))
    stream = []
    for l in range(DEPTH):
        for sq in seqs:
            if sq["kind"] == "s":
                for pb in range(PT_S // 4):
                    stream += [(l, n) for n in past_piece_order]
            for j in range(sq["nblk"]):
                stream += [(l, n) for n in block_piece_order()]

    with contextlib.ExitStack() as st:
        def sb(name, shape, dt=F32):
            return st.enter_context(nc.sbuf_tensor(name, list(shape), dt))

        identb = sb("identb", [128, 128], BF16)
        identf = sb("identf", [128, 128])
        hgm = sb("hgm", [128, 576], BF16)
        maskbd = sb("maskbd", [128, 128], BF16)
        cosT = sb("cosT", [128, NPT, 16])
        sinT = sb("sinT", [128, NPT, 16])
        m01 = sb("m01", [128, 4])
        ones_f = sb("ones_f", [128, 64])
        ones_b = sb("ones_b", [128, 64], BF16)
        eps_t = sb("eps_t", [128, 1])
        r_const = Res("const")
        gmix = sb("gmix", [128, 8])
        gffn = sb("gffn", [128, 8])
        gq = sb("gq", [128, 3])
        gkv_bc = sb("gkv_bc", [128, KVL])
        lb_bc = sb("lb_bc", [128, 512])
        ghg2 = sb("ghg2", [128, 1])
        gdf = sb("gdf", [128, 1])
        dlam = sb("dlam", [128, 4, DDH])
        lam2 = sb("lam2", [128, 4])
        convw = sb("convw", [128, NFC, 3])
        convb = sb("convb", [128, NFC])
        r_lp = Res("layerparams")
        kTm = sb("kTm", [128, MH, KMAX], BF16)
        Vm = sb("Vm", [128, KTMAX + 1, MH, 65], BF16)
        kTd = sb("kTd", [128, 4, KMAX], BF16)
        Vd = sb("Vd", [128, KTMAX + 1, DH, 65], BF16)
        r_kTm = [Res("kTm%d" % i) for i in range(KTMAX)]
        r_Vm = [Res("Vm%d" % i) for i in range(KTMAX)]
        r_kTd = [Res("kTd%d" % i) for i in range(KTMAX)]
        r_Vd = [Res("Vd%d" % i) for i in range(KTMAX)]
        S32 = sb("S32", [128, 4, 64])
        Sbf = sb("Sbf", [128, 4, 64], BF16)
        r_S32, r_Sbf = Res("S32"), Res("Sbf")
        halo = sb("halo", [128, 2, NFC])
        r_halo = [Res("halo%d" % c) for c in range(NFC)]
        Xb = sb("Xb", [128, 4, D])
        r_Xb = [Res("Xb%d" % t) for t in range(4)]
        hT = sb("hT", [128, 8, 512], BF16)
        r_hT = [Res("hT%d" % t) for t in range(4)]
        NSLOT = 8
        wring = Ring([sb("wr%d" % i, [128, 2048], BF16) for i in range(NSLOT)], "wr")
        SA = sb("SA", [128, 20, 512], BF16)
        r_SA = [Res("SA%d" % i) for i in range(20)]
        dqT0 = sb("dqT0", [128, 4, 512], BF16)
        dqT1 = sb("dqT1", [128, 4, 512], BF16)
        r_dqT = [Res("dqT%d" % c) for c in range(4)]
        r_dqv = Res("dqv")
        Uf = Xb[:, :, :].rearrange("p t d -> p (t d)")
        Ub = Uf.bitcast(BF16)

        def uview(off, shape, dt):
            n = int(np.prod(shape))
            if dt == BF16:
                v = Ub[:, off // 2:off // 2 + n]
            else:
                v = Uf[:, off // 4:off // 4 + n]
            if len(shape) == 2:
                return v.rearrange("p (a b) -> p a b", b=shape[1])
            return v

        qlatT = uview(0, [3, 512], BF16)
        r_qlatT = [Res("qlatT%d" % t) for t in range(4)]
        qtm_b = [uview(3072, [8, 96], BF16), uview(7680, [8, 96], BF16)]
        r_qtm_b = [Res("qtm0"), Res("qtm1")]
        qr_b = [uview(4608, [8, 32], F32), uview(9216, [8, 32], F32)]
        r_qr_b = [Res("qr0"), Res("qr1")]
        ckvT = uview(5632, [2, 512], BF16)
        r_ckvT = [Res("ckvT%d" % t) for t in range(4)]
        EX = uview(0, [4, 448], BF16)
        r_EX = Res("EX")
        hgp = Ring([uview(3584 + 1024 * i, [4, 128], BF16) for i in range(4)], "hgp")
        att_sb = uview(7680, [8, 128], BF16)
        r_att = Res("att")
        Pr = Ring([uview(1024 * i, [512], BF16) for i in range(3)], "Pr")
        rcp = uview(3072, [2, 512], F32)
        rcph = uview(7168, [8, 512], BF16)
        r_rcp = [Res("rcp0"), Res("rcp1")]
        r_rcph = [Res("rcph%d" % i) for i in range(4)]
        PH_QKV = r_qlatT + r_qtm_b + r_qr_b + r_ckvT
        PH_HG = [r_EX, r_att] + hgp.res
        PH_ATT = Pr.res + r_rcp + r_rcph
        fdummy = sb("fdummy", [128, 1])
        r_fd = Res("fdummy")

        def ufence(new):
            allr = r_Xb + PH_QKV + PH_HG + PH_ATT
            P.op("pool", lambda e: e.memset(fdummy[:, 0:1], 0.0), (), allr + [r_fd])
        stg = Ring([sb("stg%d" % i, [128, 512]) for i in range(3)], "stg")
        hb = Ring([sb("hb%d" % i, [128, D], BF16) for i in range(2)], "hb")
        sm = Ring([sb("sm%d" % i, [128, 16]) for i in range(6)], "sm")
        f32a = Ring([sb("fa%d" % i, [128, 512]) for i in range(4)], "fa")
        b16a = Ring([sb("ba%d" % i, [128, 512], BF16) for i in range(5)], "ba")
        krpad_b = [sb("krpad0", [128, 96], BF16), sb("krpad1", [128, 96], BF16)]
        r_krpad_b = [Res("krpad0"), Res("krpad1")]
        aext = Ring([dq_[:, :, :].rearrange("p c n -> p (c n)").bitcast(F32)[:, 0:514] for dq_ in (dqT0, dqT1)], "aext")

        pbank = [st.enter_context(nc.psum_tensor("pb%d" % i, [128, 512], F32)) for i in range(8)]
        psA = Ring(pbank[0:4], "psA", excl=True)
        psB = Ring(pbank[4:8], "psB", excl=True)

        fin = []

        def dma(out, in_, r=(), w=(), q="sp", slow=False):
            if slow:
                return P.op(q, lambda e: e.dma_start(out=out, in_=in_, allow_slow_non_contiguous=True), r, w, dma=True)
            return P.op(q, lambda e: e.dma_start(out=out, in_=in_), r, w, dma=True)

        def act(out, in_, func, r, w, **kw):
            return P.op("act", lambda e: e.activation(out=out, in_=in_, func=func, **kw), r, w)

        def tt(eng, out, in0, in1, op, r, w):
            return P.op(eng, lambda e: e.tensor_tensor(out=out, in0=in0, in1=in1, op=op), r, w)

        def ts(eng, out, in0, s1, s2, op0, op1, r, w):
            if s2 is None:
                return P.op(eng, lambda e: e.tensor_scalar(out=out, in0=in0, scalar1=s1, scalar2=None, op0=op0), r, w)
            return P.op(eng, lambda e: e.tensor_scalar(out=out, in0=in0, scalar1=s1, scalar2=s2, op0=op0, op1=op1), r, w)

        def stt(out, in0, scalar, in1, op0, op1, r, w):
            return P.op("dve", lambda e: e.scalar_tensor_tensor(out=out, in0=in0, scalar=scalar, in1=in1, op0=op0, op1=op1), r, w)

        def cp(eng, out, in_, r, w):
            if eng == "act":
                return P.op("act", lambda e: e.activation(out=out, in_=in_, func=AF.Copy), r, w)
            return P.op(eng, lambda e: e.tensor_copy(out=out, in_=in_), r, w)

        def memset(eng, ap, val, w):
            return P.op(eng, lambda e: e.memset(ap, val), (), w)

        def mm(items, r, w):
            def fn(e):
                ins = None
                for (o, a, b, s0, s1) in items:
                    ins = e.matmul(o, a, b, start=s0, stop=s1)
                return ins
            return P.op("pe", fn, r, w)

        def trs(items, r, w):
            def fn(e):
                ins = None
                for (o, a, idn) in items:
                    ins = e.transpose(o, a, idn)
                return ins
            return P.op("pe", fn, r, w)

        def dbgdump(name, ap, r):
            if name in dbg_out:
                fin.append(dma(dbg_out[name], ap, r=r))

        import os
        STOP = os.environ.get("KSTOP", "")
        SKIP = set(os.environ.get("KSKIP", "").split(","))

        class StopBuild(Exception):
            pass

        def stage(name):
            if STOP == name:
                raise StopBuild()

        for dst, src in ((identb, c_identb), (identf, c_identf), (hgm, c_hgm), (maskbd, c_maskbd),
                         (m01, c_m01)):
            dma(dst[:], src[:, :], w=[r_const])
        dma(cosT[:], c_cos[:, :, :], w=[r_const])
        dma(sinT[:], c_sin[:, :, :], w=[r_const])
        memset("pool", ones_f[:], 1.0, [r_const])
        memset("pool", ones_b[:], 1.0, [r_const])
        memset("pool", eps_t[:], EPS, [r_const])
        for i_ in range(2):
            memset("pool", krpad_b[i_][:], 0.0, [r_krpad_b[i_]])
        for kt in range(KTMAX):
            memset("pool", Vm[:, kt, :, :], 0.0, [r_Vm[kt]])
            memset("pool", Vd[:, kt, :, :], 0.0, [r_Vd[kt]])
            memset("pool", Vm[:, kt, :, 64:65], 1.0, [r_Vm[kt]])
            memset("pool", Vd[:, kt, :, 64:65], 1.0, [r_Vd[kt]])
        memset("pool", Vm[:, KTMAX, :, :], 0.0, [r_const])
        memset("pool", Vd[:, KTMAX, :, :], 0.0, [r_const])
        memset("pool", kTm[96:128, :, :], 0.0, [r_const])
        Vm_flat = Vm[:, :, :, :].rearrange("p k h c -> p (k h c)")
        Vd_flat = Vd[:, :, :, :].rearrange("p k h c -> p (k h c)")

        conv_by_layer = {}
        for dst, src, r in conv_list:
            conv_by_layer.setdefault(int(r.name[1:r.name.index("_")]), []).append((dst, src, r))
        early_names = set(block_piece_order()[:22])
        late0 = []
        for dst, src, r in conv_by_layer.get(0, []):
            if r.name[r.name.index("_") + 1:] in early_names:
                dma(dst, src, w=[r], q="pool")
            else:
                late0.append((dst, src, r))
        cast_state = {"queue": [], "per_block": 0, "late0": late0}

        def cast_some():
            for _ in range(cast_state["per_block"]):
                if cast_state["queue"]:
                    dst, src, r = cast_state["queue"].pop(0)
                    dma(dst, src, w=[r], q="pool")

        wstate = dict(issued=0, used=0, slots={}, closed=set())

        def w_issue():
            while wstate["issued"] < len(stream) and (
                    wstate["issued"] < NSLOT or (wstate["issued"] - NSLOT) in wstate["closed"]):
                i = wstate["issued"]
                l, name = stream[i]
                scr, shape, r = pieces[(l, name)]
                tile_, rs = wring.next()
                nel = int(np.prod(shape))
                view = tile_[:, 0:nel]
                if len(shape) == 2:
                    src = scr[:, :, :].rearrange("p a b -> p (a b)")
                    dma(view, src, r=[r], w=[rs])
                    view = view.rearrange("p (a b) -> p a b", b=shape[1])
                else:
                    src = scr[:, :, :, :].rearrange("p a b c -> p (a b c)")
                    dma(view, src, r=[r], w=[rs])
                    view = view.rearrange("p (a b c) -> p a b c", b=shape[1], c=shape[2])
                wstate["slots"][i] = (view, rs)
                wstate["issued"] += 1

        def W(l, name):
            i = wstate["used"]
            assert stream[i] == (l, name), (i, stream[i], (l, name))
            w_issue()
            assert wstate["issued"] > i, "weight ring deadlock at piece %d %s" % (i, name)
            wstate["used"] += 1
            v, r = wstate["slots"].pop(i)
            return (v, r, i)

        def Wc(*ws):
            for w_ in ws:
                wstate["closed"].add(w_[2])
            w_issue()

        def load_layer_params(l):
            wl = [r_lp]
            dma(gmix[:], norm_mix_g[l].rearrange("(kc p) -> p kc", p=128), w=wl, slow=True)
            dma(gffn[:], norm_ffn_g[l].rearrange("(kc p) -> p kc", p=128), w=wl, slow=True)
            dma(gq[:], mla_q_norm_g[l].rearrange("(kc p) -> p kc", p=128), w=wl, slow=True)
            dma(gkv_bc[:], mla_kv_norm_g[l].partition_broadcast(128), w=wl)
            dma(ghg2[0:64, :], hgrn_norm_g[l].rearrange("(d o) -> d o", o=1), w=wl)
            dma(ghg2[64:128, :], hgrn_norm_g[l].rearrange("(d o) -> d o", o=1), w=wl)
            dma(dlam[:], diff_lambda[l].rearrange("a c -> (a c)").partition_broadcast(128).rearrange("p (a c) -> p a c", c=DDH), w=wl)
            dma(gdf[0:64, :], diff_norm_g[l].rearrange("(d o) -> d o", o=1), w=wl)
            dma(gdf[64:128, :], diff_norm_g[l].rearrange("(d o) -> d o", o=1), w=wl)
            for jj in range(3):
                dma(convw[:, :, jj], ffn_conv_w[l, jj].rearrange("(c p) -> p c", p=128), w=wl, slow=True)
            dma(convb[:], ffn_conv_b[l].rearrange("(c p) -> p c", p=128), w=wl, slow=True)
            fm, rfm = f32a.next()
            fd, rfd = f32a.next()
            for i in range(DEPTH):
                tg, rtg = stg.next()
                dma(tg[:], hgrn_lb_logits[i].partition_broadcast(128), w=[rtg])
                if i == 0:
                    cp("dve", fm[:], tg[:], [rtg], [rfm])
                else:
                    tt("dve", fm[:], fm[:], tg[:], ALU.max, [rfm, rtg], [rfm])
            memset("pool", lb_bc[:], 0.0, wl)
            for i in range(DEPTH):
                tg, rtg = stg.next()
                dma(tg[:], hgrn_lb_logits[i].partition_broadcast(128), w=[rtg])
                tt("dve", tg[:], tg[:], fm[:], ALU.subtract, [rfm, rtg], [rtg])
                act(tg[:], tg[:], AF.Exp, [rtg], [rtg])
                if i == 0:
                    cp("dve", fd[:], tg[:], [rtg], [rfd])
                else:
                    tt("dve", fd[:], fd[:], tg[:], ALU.add, [rfd, rtg], [rfd])
                if 1 <= i <= l:
                    tt("dve", lb_bc[:], lb_bc[:], tg[:], ALU.add, [rtg] + wl, wl)
            P.op("dve", lambda e: e.reciprocal(out=fd[:], in_=fd[:]), [rfd], [rfd])
            tt("dve", lb_bc[:], lb_bc[:], fd[:], ALU.mult, [rfd] + wl, wl)
            lam_init = 0.8 - 0.6 * math.exp(-0.3 * l)
            tt("dve", dlam[:, 0, :], dlam[:, 0, :], dlam[:, 1, :], ALU.mult, wl, wl)
            tt("dve", dlam[:, 2, :], dlam[:, 2, :], dlam[:, 3, :], ALU.mult, wl, wl)
            P.op("dve", lambda e: e.reduce_sum(out=lam2[:, 0:1], in_=dlam[:, 0, :], axis=AX.X), wl, wl)
            P.op("dve", lambda e: e.reduce_sum(out=lam2[:, 1:2], in_=dlam[:, 2, :], axis=AX.X), wl, wl)
            act(lam2[:, 0:2], lam2[:, 0:2], AF.Exp, wl, wl)
            tt("dve", lam2[:, 2:3], lam2[:, 0:1], lam2[:, 1:2], ALU.subtract, wl, wl)
            ts("dve", lam2[:, 3:4], lam2[:, 2:3], -1.0, -lam_init, ALU.mult, ALU.add, wl, wl)
            ts("dve", gdf[:], gdf[:], 1.0 - lam_init, None, ALU.mult, None, wl, wl)

        def rstd(src, nt, dim, rsrc, junkbuf=None):
            junk, rj = junkbuf if junkbuf is not None else hb.next()
            s, rs = sm.next()
            act(junk[:nt, 0:dim], src, AF.Square, rsrc, [rj, rs], scale=float(dim) ** -0.5, accum_out=s[:nt, 0:1])
            act(s[:nt, 1:2], s[:nt, 0:1], AF.Ln, [rs], [rs], bias=eps_t[:nt, 0:1])
            act(s[:nt, 2:3], s[:nt, 1:2], AF.Exp, [rs], [rs], scale=-0.5)
            return s[:nt, 2:3], rs

        def norm_transpose(tiles, gcol, l, group=4):
            if group < len(tiles):
                for g0 in range(0, len(tiles), group):
                    norm_transpose(tiles[g0:g0 + group], gcol, l, group)
                return
            ntmax = max(nt for (_, nt) in tiles)
            nT = len(tiles)
            sst, rss = sm.next()
            junk, rj = hb.next()
            for i_, (t, nt) in enumerate(tiles):
                act(junk[:nt, :], Xb[:nt, t, :], AF.Square, [r_Xb[t]], [rj, rss], scale=float(D) ** -0.5,
                    accum_out=sst[:nt, i_:i_ + 1])
            act(sst[:ntmax, 4:4 + nT], sst[:ntmax, 0:nT], AF.Ln, [rss], [rss], bias=eps_t[:ntmax, 0:1])
            act(sst[:ntmax, 8:8 + nT], sst[:ntmax, 4:4 + nT], AF.Exp, [rss], [rss], scale=-0.5)
            for i_, (t, nt) in enumerate(tiles):
                h, rh = hb.next()
                r, rr = sst[:nt, 8 + i_:9 + i_], rss
                act(h[:nt, :], Xb[:nt, t, :], AF.Copy, [r_Xb[t], rr], [rh], scale=r)
                pb, rpb = psA.next()
                pbb = pb[:].bitcast(BF16)
                trs([(pbb[:, kc * 128:kc * 128 + nt], h[:nt, kc * 128:(kc + 1) * 128], identb[:nt, :nt])
                     for kc in range(8)], [rh, r_const], [rpb])
                tt("dve", hT[:, :, t * 128:t * 128 + nt],
                   pbb.rearrange("p (k c) -> p k c", c=128)[:, :, 0:nt],
                   gcol[:, :].unsqueeze(2).to_broadcast([128, 8, nt]), ALU.mult, [rpb, r_lp], [r_hT[t]])

        def group_tm(l, names, widths, tiles, consumer):
            ws = [W(l, n) for n in names]
            banks = {}

            def issue(idx):
                (t, nt) = tiles[idx]
                pb, rpb = psA.next()
                items = []
                off = 0
                for (wv, rw, _), n in zip(ws, widths):
                    for kc in range(8):
                        items.append((pb[:nt, off:off + n], hT[:, kc, t * 128:t * 128 + nt], wv[:, kc, :],
                                      kc == 0, kc == 7))
                    off += n
                mm(items, [r_hT[t]] + [w_[1] for w_ in ws], [rpb])
                banks[idx] = (pb, rpb)
                if idx == len(tiles) - 1:
                    Wc(*ws)

            issue(0)
            for idx, (t, nt) in enumerate(tiles):
                if idx + 1 < len(tiles):
                    issue(idx + 1)
                pb, rpb = banks.pop(idx)
                consumer(t, nt, pb, rpb)

        def rope_tm(src, nt, nh, ptile, rsrc, out1, out2, rout, eng="pool"):
            cosb = cosT[:nt, ptile, :].unsqueeze(1).to_broadcast([nt, nh, 16])
            sinb = sinT[:nt, ptile, :].unsqueeze(1).to_broadcast([nt, nh, 16])
            ta, rta = f32a.next()
            tb, rtb = f32a.next()
            a = ta[:nt, 0:nh * 16].rearrange("p (h c) -> p h c", c=16)
            b = ta[:nt, 256:256 + nh * 16].rearrange("p (h c) -> p h c", c=16)
            c = tb[:nt, 0:nh * 16].rearrange("p (h c) -> p h c", c=16)
            d = tb[:nt, 256:256 + nh * 16].rearrange("p (h c) -> p h c", c=16)
            x1 = src[:, :, 0:16]
            x2 = src[:, :, 16:32]
            tt(eng, a, x1, cosb, ALU.mult, rsrc + [r_const], [rta])
            tt(eng, b, x2, sinb, ALU.mult, rsrc + [r_const], [rta])
            tt(eng, c, x2, cosb, ALU.mult, rsrc + [r_const], [rtb])
            tt(eng, d, x1, sinb, ALU.mult, rsrc + [r_const], [rtb])
            tt(eng, out1, a, b, ALU.subtract, [rta], rout)
            tt(eng, out2, c, d, ALU.add, [rtb], rout)

        def mla_append_tile(l, t, nt, kt, ckv_bf, r_ckv_bf, ukv_w):
            kc0 = kt * 128
            pb, rpb = psB.next()
            pbb = pb[:].bitcast(BF16)
            trs([(pbb[:, c * 128:c * 128 + nt], ckv_bf[:nt, c * 128:(c + 1) * 128], identb[:nt, :nt]) for c in range(2)],
                [r_ckv_bf, r_const], [rpb])
            cp("act", ckvT[:, :, t * 128:t * 128 + nt], pbb[:, 0:256].rearrange("p (k c) -> p k c", c=128)[:, :, 0:nt],
               [rpb], [r_ckvT[t]])
            krpad, r_krpad = krpad_b[kt % 2], r_krpad_b[kt % 2]
            pb2, rpb2 = psB.next()
            pbb2 = pb2[:].bitcast(BF16)
            trs([(pbb2[0:96, 0:nt], krpad[:nt, :], identb[:nt, :nt])], [r_krpad, r_const], [rpb2])
            cp("dve", kTm[64:96, :, kc0:kc0 + nt], pbb2[64:96, 0:nt].unsqueeze(1).to_broadcast([32, MH, nt]),
               [rpb2], [r_kTm[kt]])
            (wv, rwv, _) = ukv_w
            pb3, rpb3 = psB.next()
            mm([(pb3[:nt, :], ckvT[:, c, t * 128:t * 128 + nt], wv[:, c, :, :].rearrange("p h c -> p (h c)"),
                 c == 0, c == 1) for c in range(2)], [r_ckvT[t], rwv], [rpb3])
            cp("act", Vm[:nt, kt, :, 0:64], pb3[:nt, :].rearrange("p (h c) -> p h c", c=64), [rpb3], [r_Vm[kt]])

        def dk_append_tile(l, t, nt, kt, dk_bf, r_dk_bf):
            kc0 = kt * 128
            pb4, rpb4 = psA.next()
            pbb4 = pb4[:].bitcast(BF16)
            trs([(pbb4[:, c * 128:c * 128 + nt], dk_bf[:nt, c * 128:(c + 1) * 128], identb[:nt, :nt]) for c in range(4)],
                [r_dk_bf, r_const], [rpb4])
            cp("dve", kTd[:, :, kc0:kc0 + nt], pbb4[:, 0:512].rearrange("p (k c) -> p k c", c=128)[:, :, 0:nt],
               [rpb4], [r_kTd[kt]])

        def knope_block(l, tiles, kt0, nb, ukk_w):
            (wk, rwk, _) = ukk_w
            rts = [r_ckvT[t] for (t, _) in tiles]
            wts = [r_kTm[kt0 + t] for (t, _) in tiles]
            for h in range(MH):
                pb, rpb = psA.next()
                mm([(pb[0:64, 0:nb], wk[:, c, h, :], ckvT[:, c, 0:nb], c == 0, c == 1) for c in range(2)],
                   rts + [rwk], [rpb])
                cp("act" if h % 2 == 0 else "dve", kTm[0:64, h, kt0 * 128:kt0 * 128 + nb], pb[0:64, 0:nb], [rpb], wts)

        def attention(kind, tiles, nb, kts, diag0, brbase):
            nmap = 1 if kind == "m" else 2
            psS = Ring(psA.tiles[0:3], "psS")
            psS.res = psA.res[0:3]
            psF, rpsF = psA.tiles[3], psA.res[3]
            steps = []
            for h in range(8):
                for (kt, nk) in kts:
                    for j in range(nmap):
                        steps.append((h, kt, nk, j))
            LOOK = 2
            sc_q = {}
            pending = []
            state = {}
            kt_first, kt_last = kts[0][0], kts[-1][0]
            Vbuf, rV = (Vm_flat, r_Vm) if kind == "m" else (Vd_flat, r_Vd)
            scale = MLA_SCALE if kind == "m" else DF_SCALE

            def run_pending(upto, head_le=None):
                keep = []
                for (due, hd, fn) in pending:
                    if (upto is not None and due <= upto) or (head_le is not None and hd <= head_le) or \
                            (upto is None and head_le is None):
                        fn()
                    else:
                        keep.append((due, hd, fn))
                pending[:] = keep

            def issue(i):
                h, kt, nk, j = steps[i]
                if kind == "d" and h % 2 == 0 and kt == kt_first and j == 0:
                    c = h // 2
                    for v in range(4):
                        ts("dve", dqT1[:, v, 0:nb], dqT0[:, c, 0:nb], m01[:, v:v + 1], None, ALU.mult, None,
                           [r_dqT[c], r_const], [r_dqv])
                qlo = 0 if (diag0 is None or kt < diag0) else 128 * (kt - diag0)
                pb, rpb = psS.next()
                if kind == "m":
                    lhsT = kTm[:, h, kt * 128:kt * 128 + nk]
                    rhs = SA[:, 12 + h, qlo:nb]
                    rd = [r_kTm[kt], r_SA[12 + h], r_const]
                else:
                    lhsT = kTd[:, h // 2, kt * 128:kt * 128 + nk]
                    rhs = dqT1[:, (h % 2) * 2 + j, qlo:nb]
                    rd = [r_kTd[kt], r_dqv]
                mm([(pb[:nk, qlo:nb], lhsT, rhs, True, True)], rd, [rpb])
                sc_q[i] = (pb, rpb, qlo)

            for i in range(min(LOOK, len(steps))):
                issue(i)
            for i, (h, kt, nk, j) in enumerate(steps):
                run_pending(i)
                if i + LOOK < len(steps):
                    issue(i + LOOK)
                pb, rpb, qlo = sc_q.pop(i)
                first = (kt == kt_first)
                last = (kt == kt_last)
                if first and j == 0:
                    run_pending(None, head_le=h - 2)
                    state["ot"] = [psB.next() for _ in range(nmap)]
                pt, rpt = Pr.next()
                act(pt[:nk, qlo:nb], pb[:nk, qlo:nb], AF.Exp, [rpb], [rpt], scale=scale)
                if diag0 is not None and kt >= diag0:
                    memset("pool", pt[64:128, qlo:qlo + 64], 0.0, [rpt])
                ot, rot = state["ot"][j]
                v0 = kt * 520 + h * 65
                rdv = [rV[kt], rpt, r_const] + ([rV[kt + 1]] if (h == 7 and kt + 1 < KTMAX) else [])
                mm([(ot[:, qlo:nb], Vbuf[:nk, v0:v0 + 128], pt[:nk, qlo:nb], first, last)], rdv, [rot])
                if last and j == nmap - 1:
                    run_pending(None, head_le=h - 2)
                    base = max(i, state.get("fin_base", 0))
                    stages, busy = finish_head(kind, h, nb, state["ot"], brbase, psF, rpsF)
                    for (d_, fn) in stages:
                        pending.append((base + d_, h, fn))
                    state["fin_base"] = base + busy
            run_pending(None)

        def finish_head(kind, h, nb, ots, brbase, psF, rpsF):
            slot = brbase + h // 2
            hb_ = (h % 2) * 64
            dst = SA[hb_:hb_ + 64, slot, 0:nb]
            rows = [((h % 2) * 2 + j if kind == "d" else h % 4) for j in range(len(ots))]
            pre = []
            for j, (ot, rot) in enumerate(ots):
                jj = rows[j]

                def r_ln(j=j, ot=ot, rot=rot):
                    act(rcp[64:65, j, 0:nb], ot[64:65, 0:nb], AF.Ln, [rot], [r_rcp[j]])

                def r_exp(j=j, jj=jj):
                    act(rcp[64:65, j, 0:nb], rcp[64:65, j, 0:nb], AF.Exp, [r_rcp[j]], [r_rcp[j]], scale=-1.0)
                    if kind == "d" and j == 1:
                        ts("dve", rcp[64:65, 1, 0:nb], rcp[64:65, 1, 0:nb], lam2[64:65, 3:4], None, ALU.mult, None,
                           [r_rcp[1], r_lp], [r_rcp[1]])
                    cp("dve", rcph[64:65, jj * 2, 0:nb], rcp[64:65, j, 0:nb], [r_rcp[j]], [r_rcph[jj]])
                    tt("dve", rcph[64:65, jj * 2 + 1, 0:nb], rcp[64:65, j, 0:nb], rcph[64:65, jj * 2, 0:nb], ALU.subtract,
                       [r_rcp[j], r_rcph[jj]], [r_rcph[jj]])
                pre += [(1 + 2 * j, r_ln), (2 + 2 * j, r_exp)]
            off = 2 * len(ots)
            A, rA = f32a.next()

            def bc_mm(jj):
                mm([(psF[0:64, 0:nb], ones_b[64:65, 0:64], rcph[64:65, jj * 2, 0:nb], True, False),
                    (psF[0:64, 0:nb], ones_b[64:65, 0:64], rcph[64:65, jj * 2 + 1, 0:nb], False, True)],
                   [r_rcph[jj], r_const], [rpsF])

            if kind == "m":
                ot, rot = ots[0]

                def m2():
                    bc_mm(rows[0])

                def m4():
                    cp("dve", A[0:64, 0:nb], psF[0:64, 0:nb], [rpsF], [rA])
                    tt("dve", dst, ot[0:64, 0:nb], A[0:64, 0:nb], ALU.mult, [rot, rA], [r_SA[slot]])
                return pre + [(off + 2, m2), (off + 4, m4)], off + 5
            B, rB = f32a.next()
            (ot0, rot0), (ot1, rot1) = ots
            st_ = {}

            def d2():
                bc_mm(rows[0])

            def d4():
                cp("dve", A[0:64, 0:nb], psF[0:64, 0:nb], [rpsF], [rA])
                bc_mm(rows[1])

            def d6():
                tt("dve", A[0:64, 0:nb], ot0[0:64, 0:nb], A[0:64, 0:nb], ALU.mult, [rot0, rA], [rA])
                cp("dve", B[0:64, 0:nb], psF[0:64, 0:nb], [rpsF], [rB])
                tt("dve", B[0:64, 0:nb], ot1[0:64, 0:nb], B[0:64, 0:nb], ALU.mult, [rot1, rB], [rB])
                tt("dve", A[0:64, 0:nb], A[0:64, 0:nb], B[0:64, 0:nb], ALU.add, [rA, rB], [rA])
                st_["sq"] = b16a.next()
                sq, rsq = st_["sq"]
                tt("pool", sq[0:64, 0:nb], A[0:64, 0:nb], A[0:64, 0:nb], ALU.mult, [rA], [rsq])

            def d9():
                sq, rsq = st_["sq"]
                mm([(psF[0:64, 0:nb], ones_b[0:64, 0:64], sq[0:64, 0:nb], True, True)], [rsq, r_const], [rpsF])

            def d11():
                act(B[0:64, 0:nb], psF[0:64, 0:nb], AF.Ln, [rpsF, r_const], [rB], scale=1.0 / DDV, bias=eps_t[0:64, 0:1])
                act(B[0:64, 0:nb], B[0:64, 0:nb], AF.Exp, [rB], [rB], scale=-0.5)

            def d13():
                stt(dst, A[0:64, 0:nb], gdf[0:64, 0:1], B[0:64, 0:nb], ALU.mult, ALU.mult, [rA, rB, r_lp], [r_SA[slot]])
            return pre + [(off + 2, d2), (off + 4, d4), (off + 6, d6), (off + 8, d9), (off + 10, d11), (off + 12, d13)], off + 11

        def hgrn_tile(l, t, nt, banks, pre_next=None):
            (pq, rpq), (pf, rpf), (pi, rpi), (pg, rpg) = banks
            qh, rqh = b16a.next()
            f, rf = f32a.next()
            lf, rlf = f32a.next()
            gs, rgs = f32a.next()
            act(lf[:nt, :], pq[:nt, :], AF.Sigmoid, [rpq], [rlf])
            act(gs[:nt, :], pg[:nt, :], AF.Sigmoid, [rpg], [rgs])
            act(f[:nt, :], pf[:nt, :], AF.Sigmoid, [rpf], [rf])
            tt("dve", qh[:nt, :], pq[:nt, :], lf[:nt, :], ALU.mult, [rpq, rlf], [rqh])
            tt("dve", gs[:nt, :], pg[:nt, :], gs[:nt, :], ALU.mult, [rpg, rgs], [rgs])
            kb, rkb = b16a.next()
            stt(lf[:nt, :], f[:nt, :], -1.0, lb_bc[:nt, :], ALU.add, ALU.mult, [rf, r_lp], [rlf])
            tt("dve", f[:nt, :], f[:nt, :], lf[:nt, :], ALU.subtract, [rf, rlf], [rf])
            ts("dve", kb[:nt, :], f[:nt, :], -1.0, 1.0, ALU.mult, ALU.add, [rf], [rkb])
            ts("dve", lf[:nt, :], f[:nt, :], F_MIN, None, ALU.max, None, [rf, rlf], [rlf])
            vb, rvb = b16a.next()
            cp("dve", vb[:nt, :], pi[:nt, :], [rpi], [rvb])
            pq2, rpq2 = psA.next()
            pqb = pq2[:].bitcast(BF16)
            trs([(pqb[:, c * 128:c * 128 + nt], qh[:nt, c * 128:(c + 1) * 128], identb[:nt, :nt]) for c in range(4)],
                [rqh, r_const], [rpq2])
            pk2, rpk2 = psA.next()
            pkb = pk2[:].bitcast(BF16)
            trs([(pkb[:, c * 128:c * 128 + nt], kb[:nt, c * 128:(c + 1) * 128], identb[:nt, :nt]) for c in range(4)],
                [rkb, r_const], [rpk2])
            qT = pqb[:, 0:512].rearrange("p (k c) -> p k c", c=128)
            kT = pkb[:, 0:512].rearrange("p (k c) -> p k c", c=128)
            lfh, rlfh = b16a.next()
            lfl, rlfl = b16a.next()
            act(lfh[:nt, :], lf[:nt, :], AF.Ln, [rlf], [rlfh])
            act(lf[:nt, :], lf[:nt, :], AF.Ln, [rlf], [rlf])
            tt("dve", lfl[:nt, :], lf[:nt, :], lfh[:nt, :], ALU.subtract, [rlf, rlfh], [rlfl])
            pe1, rpe1 = psB.next()
            mm([(pe1[:nt, :], hgm[:nt, 448:448 + nt], lfh[:nt, :], True, False),
                (pe1[:nt, :], hgm[:nt, 448:448 + nt], lfl[:nt, :], False, True)], [rlfh, rlfl, r_const], [rpe1])
            ekk, rekk = f32a.next()
            act(ekk[:nt, :], pe1[:nt, :], AF.Exp, [rpe1], [rekk])
            kk, rkk = b16a.next()
            tt("dve", kk[:nt, :], kb[:nt, :], ekk[:nt, :], ALU.mult, [rkb, rekk], [rkk])
            for c in range(4):
                pe2, rpe2 = psB.next()
                mm([(pe2[:, 0:448], lfh[:nt, c * 128:(c + 1) * 128], hgm[:nt, 0:448], True, False),
                    (pe2[:, 0:448], lfl[:nt, c * 128:(c + 1) * 128], hgm[:nt, 0:448], False, True)],
                   [rlfh, rlfl, r_const], [rpe2])
                act(EX[:, c, :], pe2[:, 0:448], AF.Exp, [rpe2], [r_EX])
            Qt, rQt = hgp.next()
            Ko, rKo = hgp.next()
            Qb, rQb = hgp.next()
            Kc, rKc = hgp.next()
            tt("dve", Qt[:, :, 0:nt], qT[:, :, 0:nt], EX[:, :, 0:nt], ALU.mult, [rpq2, r_EX], [rQt])
            tt("dve", Ko[:, :, 0:nt], kT[:, :, 0:nt], EX[:, :, 128:128 + nt], ALU.mult, [rpk2, r_EX], [rKo])
            tt("dve", Qb[:, :, 0:nt], qT[:, :, 0:nt], EX[:, :, 256:256 + nt], ALU.mult, [rpq2, r_EX], [rQb])
            cross = nt > 64
            if cross:
                tt("dve", Kc[:, :, 0:64], kT[:, :, 0:64], EX[:, :, 384:448], ALU.mult, [rpk2, r_EX], [rKc])
            pas = [psA.next(), psA.next()]
            items = []
            for h in range(8):
                c, b0, i2 = h // 2, (h % 2) * 64, h // 2
                bank = pas[h % 2][0]
                items.append((bank[:nt, i2 * 128:i2 * 128 + nt], Ko[b0:b0 + 64, c, 0:nt], Qt[b0:b0 + 64, c, 0:nt], True, True))
            mm(items, [rKo, rQt], [pas[0][1], pas[1][1]])
            for par in range(2):
                bank, rb = pas[par]
                tt("dve", att_sb[:nt, par * 4:par * 4 + 4, 0:nt],
                   bank[:nt, :].rearrange("p (h c) -> p h c", c=128)[:, :, 0:nt],
                   maskbd[:nt, 0:nt].unsqueeze(1).to_broadcast([nt, 4, nt]), ALU.mult, [rb, r_const], [r_att])
            if cross:
                pcs = [psA.next(), psA.next()]
                items = []
                for h in range(8):
                    c, b0, i2 = h // 2, (h % 2) * 64, h // 2
                    bank = pcs[h % 2][0]
                    items.append((bank[0:64, i2 * 64:(i2 + 1) * 64], Kc[b0:b0 + 64, c, 0:64], Qt[b0:b0 + 64, c, 64:128], True, True))
                mm(items, [rKc, rQt], [pcs[0][1], pcs[1][1]])
                for par in range(2):
                    bank, rb = pcs[par]
                    cp("act", att_sb[0:64, par * 4:par * 4 + 4, 64:128], bank[0:64, 0:256].rearrange("p (h c) -> p h c", c=64),
                       [rb], [r_att])
            po, rpo = psB.next()
            items = []
            for h in range(8):
                c, b0 = h // 2, (h % 2) * 64
                items.append((po[:nt, h * 64:(h + 1) * 64], att_sb[:nt, (h % 2) * 4 + h // 2, 0:nt], vb[:nt, h * 64:(h + 1) * 64], True, False))
                items.append((po[:nt, h * 64:(h + 1) * 64], Qb[b0:b0 + 64, c, 0:nt], Sbf[b0:b0 + 64, c, :], False, True))
            mm(items, [r_att, rvb, rQb, r_Sbf], [rpo])
            pd, rpd = psB.next()
            mm([(pd[:, c * 128:(c + 1) * 128], kk[:nt, c * 128:(c + 1) * 128], vb[:nt, c * 128:(c + 1) * 128], True, True)
                for c in range(4)], [rkk, rvb], [rpd])
            if pre_next is not None:
                pre_next()
            sq, rsq = f32a.next()
            act(sq[:nt, :], po[:nt, :], AF.Square, [rpo], [rsq], scale=float(HDV) ** -0.5)
            s, rs = sm.next()
            P.op("dve", lambda e: e.reduce_sum(out=s[:nt, 0:8], in_=sq[:nt, :].rearrange("p (h c) -> p h c", c=64), axis=AX.X),
                 [rsq], [rs])
            act(s[:nt, 0:8], s[:nt, 0:8], AF.Ln, [rs, r_const], [rs], bias=eps_t[:nt, 0:1])
            act(s[:nt, 8:16], s[:nt, 0:8], AF.Exp, [rs], [rs], scale=-0.5)
            tt("dve", sq[:nt, :].rearrange("p (h c) -> p h c", c=64), po[:nt, :].rearrange("p (h c) -> p h c", c=64),
               s[:nt, 8:16].unsqueeze(2).to_broadcast([nt, 8, 64]), ALU.mult, [rpo, rs], [rsq])
            ob, rob = b16a.next()
            tt("dve", ob[:nt, :], sq[:nt, :], gs[:nt, :], ALU.mult, [rsq, rgs], [rob])
            pt_, rpt_ = psA.next()
            ptb = pt_[:].bitcast(BF16)
            trs([(ptb[:, c * 128:c * 128 + nt], ob[:nt, c * 128:(c + 1) * 128], identb[:nt, :nt]) for c in range(4)],
                [rob, r_const], [rpt_])
            act(SA[:, 4:8, t * 128:t * 128 + nt], ptb[:, 0:512].rearrange("p (k c) -> p k c", c=128)[:, :, 0:nt], AF.Copy,
                [rpt_, r_lp], [r_SA[4], r_SA[5], r_SA[6], r_SA[7]], scale=ghg2[:, 0:1])
            dcol = 256 + nt - 1
            for c in range(4):
                for b0 in (0, 64):
                    P.op("dve", lambda e, c=c, b0=b0: e.scalar_tensor_tensor(
                        out=S32[b0:b0 + 64, c, :], in0=S32[b0:b0 + 64, c, :], scalar=EX[b0:b0 + 64, c, dcol:dcol + 1],
                        in1=pd[b0:b0 + 64, c * 128 + b0:c * 128 + b0 + 64], op0=ALU.mult, op1=ALU.add),
                        [r_S32, r_EX, rpd, r_Sbf], [r_S32])
            cp("act", Sbf[:], S32[:], [r_S32], [r_Sbf])

        def do_block(l, sq, j):
            is_s = sq["kind"] == "s"
            si = sq["idx"]
            if is_s:
                tiles = [(0, TS)]
                nb = TS
                kt0 = PT_S
                tok0 = 0
            else:
                tiles = [(t, 128) for t in range(4)]
                nb = 512
                kt0 = 4 * j
                tok0 = 512 * j
            last_layer = (l == DEPTH - 1)

            def load_x():
                for (t, nt) in tiles:
                    if l == 0:
                        src = x_sample[0, 0:nt, :] if is_s else x_prompt[si, tok0 + t * 128:tok0 + t * 128 + nt, :]
                        dma(Xb[:nt, t, :], src, w=[r_Xb[t]])
                    else:
                        dma(Xb[:nt, t, :], xs[si, tok0 + t * 128:tok0 + t * 128 + nt, :], r=[r_xs[si][j]], w=[r_Xb[t]])
            if is_s:
                ufence(r_Xb)
            load_x()
            norm_transpose(tiles, gmix, l, group=2)
            dbgdump("hT", hT[:, :, :], r_hT)
            stage("A")
            ufence(PH_QKV)

            def cons_q(t, nt, pb, rpb):
                h, rh = hb.next()
                r, rr = rstd(pb[:nt, 0:QL], nt, QL, [rpb], (h, rh))
                act(h[:nt, 0:QL], pb[:nt, 0:QL], AF.Copy, [rpb, rr], [rh], scale=r)
                pt_, rpt_ = psA.next()
                ptb = pt_[:].bitcast(BF16)
                trs([(ptb[:, c * 128:c * 128 + nt], h[:nt, c * 128:(c + 1) * 128], identb[:nt, :nt]) for c in range(3)],
                    [rh, r_const], [rpt_])
                tt("dve", qlatT[:, :, t * 128:t * 128 + nt], ptb[:, 0:384].rearrange("p (k c) -> p k c", c=128)[:, :, 0:nt],
                   gq[:, :].unsqueeze(2).to_broadcast([128, 3, nt]), ALU.mult, [rpt_, r_lp], [r_qlatT[t]])
            memset("pool", SA[96:128, 12:20, :], 0.0, [r_SA[12 + h] for h in range(8)])
            group_tm(l, ["q0", "q1"], [256, 128], tiles, cons_q)
            uq0 = W(l, "uq0")
            uq1 = W(l, "uq1")
            uqb = {}

            def issue_uq(idx):
                (t, nt) = tiles[idx]
                lst = []
                for half, (wv, rw, _) in enumerate((uq0, uq1)):
                    pb, rpb = psA.next()
                    mm([(pb[:nt, 0:384], qlatT[:, c, t * 128:t * 128 + nt], wv[:, c, :], c == 0, c == 2) for c in range(3)],
                       [r_qlatT[t], rw], [rpb])
                    lst.append((pb, rpb))
                uqb[idx] = lst

            issue_uq(0)
            for idx, (t, nt) in enumerate(tiles):
                qtm, r_qtm, qr, r_qr = qtm_b[t % 2], r_qtm_b[t % 2], qr_b[t % 2], r_qr_b[t % 2]
                for half, (pb, rpb) in enumerate(uqb.pop(idx)):
                    pv = pb[:nt, 0:384].rearrange("p (h c) -> p h c", c=96)
                    cp("act", qtm[:nt, half * 4:half * 4 + 4, 0:64], pv[:, :, 0:64], [rpb], [r_qtm])
                    cp("dve", qr[:nt, half * 4:half * 4 + 4, :], pv[:, :, 64:96], [rpb], [r_qr])
                if idx + 1 < len(tiles):
                    issue_uq(idx + 1)
                rope_tm(qr[:nt, :, :], nt, 8, kt0 + t, [r_qr], qtm[:nt, :, 64:80], qtm[:nt, :, 80:96], [r_qtm], eng="dve")
                pt_, rpt_ = psA.next()
                ptb = pt_[:].bitcast(BF16)
                trs([(ptb[0:96, h * 128:h * 128 + nt], qtm[:nt, h, :], identb[:nt, :nt]) for h in range(8)],
                    [r_qtm, r_const], [rpt_])
                cp("dve", SA[0:96, 12:20, t * 128:t * 128 + nt],
                   ptb[0:96, :].rearrange("p (h c) -> p h c", c=128)[:, :, 0:nt], [rpt_], [r_SA[12 + h] for h in range(8)])
            Wc(uq0, uq1)
            stage("q")

            ukv_w = W(l, "ukv")
            ukk_w = W(l, "ukk")

            def cons_kv(t, nt, pb, rpb):
                qr, r_qr = qr_b[t % 2], r_qr_b[t % 2]
                krpad, r_krpad = krpad_b[(kt0 + t) % 2], r_krpad_b[(kt0 + t) % 2]
                r, rr = rstd(pb[:nt, 0:KVL], nt, KVL, [rpb])
                sg_, rsg = stg.next()
                stt(sg_[:nt, 0:KVL], pb[:nt, 0:KVL], r, gkv_bc[:nt, :], ALU.mult, ALU.mult, [rpb, rr, r_lp], [rsg])
                cp("act", qr[:nt, 0, :], pb[:nt, KVL:KVL + MR], [rpb], [r_qr])
                if "krrope" not in SKIP:
                    rope_tm(qr[:nt, 0:1, :], nt, 1, kt0 + t, [r_qr],
                            sg_[:nt, 256:272].unsqueeze(1), sg_[:nt, 272:288].unsqueeze(1), [rsg], eng="dve")
                oc, okr = (s_ckv, s_krope) if is_s else (p_ckv, p_krope)
                bi = 0 if is_s else si
                r0 = tok0 + t * 128
                if "kvout" not in SKIP:
                    fin.append(dma(oc[l, bi, r0:r0 + nt, :], sg_[:nt, 0:KVL], r=[rsg]))
                    fin.append(dma(okr[l, bi, r0:r0 + nt, :], sg_[:nt, 256:288], r=[rsg]))
                cb, rcb = b16a.next()
                if "kvcp" not in SKIP:
                    cp("dve", cb[:nt, 0:KVL], sg_[:nt, 0:KVL], [rsg], [rcb])
                    cp("dve", krpad[:nt, 64:96], sg_[:nt, 256:288], [rsg], [r_krpad])
                if "append" not in SKIP:
                    mla_append_tile(l, t, nt, kt0 + t, cb, rcb, ukv_w)
            group_tm(l, ["kv", "kr"], [256, 32], tiles, cons_kv)
            if "knope" not in SKIP:
                knope_block(l, tiles, kt0, nb, ukk_w)
            Wc(ukv_w, ukk_w)
            stage("kv")
            for dst_, src_, r_ in cast_state["late0"]:
                dma(dst_, src_, w=[r_], q="pool")
            cast_state["late0"] = []
            ufence(PH_HG)

            hw = [W(l, n) for n in ("hq0", "hq1", "hf0", "hf1", "hi0", "hi1", "hg0", "hg1")]
            if j == 0 or is_s:
                if is_s:
                    dma(S32[:], st_hg[l, 0].rearrange("(c two) k v -> (two k) c v", two=2), w=[r_S32])
                    pass
                else:
                    memset("pool", S32[:], 0.0, [r_S32])
                cp("act", Sbf[:], S32[:], [r_S32], [r_Sbf])
            def issue_groups(idx):
                (t, nt) = tiles[idx]
                banks = []
                for g in range(4):
                    pb, rpb = (psB if g >= 2 else psA).next()
                    items = []
                    for half in range(2):
                        wv, rw, _ = hw[g * 2 + half]
                        for kc in range(8):
                            items.append((pb[:nt, half * 256:(half + 1) * 256], hT[:, kc, t * 128:t * 128 + nt], wv[:, kc, :],
                                          kc == 0, kc == 7))
                    mm(items, [r_hT[t], hw[g * 2][1], hw[g * 2 + 1][1]], [rpb])
                    if idx == len(tiles) - 1:
                        Wc(hw[g * 2], hw[g * 2 + 1])
                    banks.append((pb, rpb))
                return banks

            hstate = {"next": issue_groups(0)}
            for idx, (t, nt) in enumerate(tiles):
                banks = hstate["next"]

                def pre(idx=idx):
                    if idx + 1 < len(tiles):
                        hstate["next"] = issue_groups(idx + 1)
                hgrn_tile(l, t, nt, banks, pre)
            if (not is_s and j == sq["nblk"] - 1) or is_s:
                oh = s_hgrn if is_s else p_hgrn
                bi = 0 if is_s else si
                fin.append(dma(oh[l, bi].rearrange("(c two) k v -> (two k) c v", two=2), S32[:], r=[r_S32]))

            stage("hg")
            P.op("pool", lambda e: e.memset(fdummy[:, 0:1], 0.0), (), aext.res + r_dqT + [r_dqv, r_fd])
            for half in range(2):
                wq_ = W(l, "dq%d" % half)
                wv, rw, _ = wq_
                for m in range(2):
                    c = half * 2 + m
                    pb, rpb = psA.next()
                    mm([(pb[:, 0:nb], wv[:, kc, m * 128:(m + 1) * 128], hT[:, kc, 0:nb], kc == 0, kc == 7) for kc in range(8)],
                       [rw] + [r_hT[t] for (t, _) in tiles], [rpb])
                    cp("act", dqT0[:, c, 0:nb], pb[:, 0:nb], [rpb], [r_dqT[c]])
                Wc(wq_)
                Wc(wq_)

            def cons_dk(t, nt, pb, rpb):
                sg_, rsg = stg.next()
                cp("act", sg_[:nt, :], pb[:nt, :], [rpb], [rsg])
                od = s_dk if is_s else p_dk
                bi = 0 if is_s else si
                r0 = tok0 + t * 128
                fin.append(dma(od[l, bi, r0:r0 + nt, :], sg_[:nt, :], r=[rsg]))
                db, rdb = b16a.next()
                cp("dve", db[:nt, :], pb[:nt, :], [rpb], [rdb])
                dk_append_tile(l, t, nt, kt0 + t, db, rdb)
            group_tm(l, ["dk0", "dk1"], [256, 256], tiles, cons_dk)

            def cons_dv(t, nt, pb, rpb):
                sg_, rsg = stg.next()
                cp("act", sg_[:nt, :], pb[:nt, :], [rpb], [rsg])
                od = s_dv if is_s else p_dv
                bi = 0 if is_s else si
                r0 = tok0 + t * 128
                fin.append(dma(od[l, bi, r0:r0 + nt, :], sg_[:nt, :], r=[rsg]))
                cp("dve", Vd[:nt, kt0 + t, :, 0:64], pb[:nt, :].rearrange("p (h c) -> p h c", c=64), [rpb], [r_Vd[kt0 + t]])
            group_tm(l, ["dv0", "dv1"], [256, 256], tiles, cons_dv)

            stage("dv")
            ufence(PH_ATT)
            if is_s:
                kts = [(kt, 128) for kt in range(PT_S)] + [(PT_S, TS)]
                diag0 = None
            else:
                kts = [(kt, 128) for kt in range(kt0 + 4)]
                diag0 = kt0
            attention("m", tiles, nb, kts, diag0, 0)
            stage("attm")
            attention("d", tiles, nb, kts, diag0, 8)
            stage("attd")
            ufence(r_Xb)
            load_x()
            dbgdump("brT", SA[:, 0:12, :], r_SA[0:12])

            psA.i = 0
            for d in range(8):
                wgA, wgB, wbr = W(l, "gA%d" % d), W(l, "gB%d" % d), W(l, "br%d" % d)
                (gA, rgA, _), (gB, rgB, _), (br, rbr, _) = wgA, wgB, wbr
                acc, racc = f32a.next()
                rts = [r_hT[t] for (t, _) in tiles]
                for n in range(3):
                    pg_, rpg_ = psA.next()
                    mm([(pg_[:, 0:nb], (gA[:, kc, n, :] if n < 2 else gB[:, kc, :]), hT[:, kc, 0:nb], kc == 0, kc == 7)
                        for kc in range(8)], rts + [rgA if n < 2 else rgB], [rpg_])
                    pp, rpp = psB.next()
                    mm([(pp[:, 0:nb], br[:, kc, n, :], SA[:, n * 4 + kc, 0:nb], kc == 0, kc == 3) for kc in range(4)],
                       [rbr] + [r_SA[n * 4 + kc] for kc in range(4)], [rpp])
                    gs, rgs = f32a.next()
                    act(gs[:, 0:nb], pg_[:, 0:nb], AF.Sigmoid, [rpg_], [rgs])
                    if n == 0:
                        tt("dve", acc[:, 0:nb], gs[:, 0:nb], pp[:, 0:nb], ALU.mult, [rgs, rpp], [racc])
                    else:
                        tt("dve", gs[:, 0:nb], gs[:, 0:nb], pp[:, 0:nb], ALU.mult, [rgs, rpp], [rgs])
                        if n == 1:
                            tt("pool", acc[:, 0:nb], acc[:, 0:nb], gs[:, 0:nb], ALU.add, [racc, rgs], [racc])
                        else:
                            tt("pool", SA[:, 12 + d, 0:nb], acc[:, 0:nb], gs[:, 0:nb], ALU.add, [racc, rgs], [r_SA[12 + d]])
                Wc(wgA, wgB, wbr)
                Wc(wgA, wgB, wbr)
            wwos = [W(l, "wo%d" % i) for i in range(4)]
            for (t, nt) in tiles:
                for half in range(2):
                    pb, rpb = psA.next()
                    items = []
                    for i2 in range(2):
                        wv, rw, _ = wwos[half * 2 + i2]
                        for kc in range(8):
                            items.append((pb[:nt, i2 * 256:(i2 + 1) * 256], SA[:, 12 + kc, t * 128:t * 128 + nt], wv[:, kc, :],
                                          kc == 0, kc == 7))
                    mm(items, [wwos[half * 2][1], wwos[half * 2 + 1][1]] + [r_SA[12 + kc] for kc in range(8)], [rpb])
                    tt("dve", Xb[:nt, t, half * 512:(half + 1) * 512], Xb[:nt, t, half * 512:(half + 1) * 512], pb[:nt, 0:512],
                       ALU.add, [rpb, r_Xb[t]], [r_Xb[t]])
            Wc(*wwos)
            dbgdump("xmid", Xb[:, :, :], r_Xb)

            stage("merge")
            P.op("pool", lambda e: e.memset(fdummy[:, 0:1], 0.0), (), aext.res + r_dqT + [r_dqv, r_fd])
            def finalize_tile(t, nt):
                if last_layer:
                    r, rr = rstd(Xb[:nt, t, :], nt, D, [r_Xb[t]])
                    for hf_ in range(2):
                        tg, rtg = stg.next()
                        dma(tg[:nt, :], norm_final_g[hf_ * 512:(hf_ + 1) * 512].partition_broadcast(nt), w=[rtg])
                        stt(Xb[:nt, t, hf_ * 512:(hf_ + 1) * 512], Xb[:nt, t, hf_ * 512:(hf_ + 1) * 512], r, tg[:nt, :],
                            ALU.mult, ALU.mult, [r_Xb[t], rr, rtg], [r_Xb[t]])
                    oy = y_sample if is_s else y_prompt
                    bi = 0 if is_s else si
                    fin.append(dma(oy[bi, tok0 + t * 128:tok0 + t * 128 + nt, :], Xb[:nt, t, :], r=[r_Xb[t]]))
                else:
                    dma(xs[si, tok0 + t * 128:tok0 + t * 128 + nt, :], Xb[:nt, t, :], r=[r_Xb[t]], w=[r_xs[si][j]])

            norm_transpose(tiles, gffn, l, group=2)
            T_end = nb
            for fh in range(2):
                for ci in range(11):
                    c = fh * 11 + ci
                    wup = W(l, "up%d" % c)
                    wv, rw, _ = wup
                    rts = [r_hT[t] for (t, _) in tiles]
                    pa, rpa = psA.next()
                    mm([(pa[:, 0:nb], wv[:, kc, 0, :], hT[:, kc, 0:nb], kc == 0, kc == 7) for kc in range(8)], rts + [rw], [rpa])
                    pv_, rpv_ = psB.next()
                    mm([(pv_[:, 0:nb], wv[:, kc, 1, :], hT[:, kc, 0:nb], kc == 0, kc == 7) for kc in range(8)], rts + [rw], [rpv_])
                    Wc(wup)
                    ae, rae = aext.next()
                    if j == 0:
                        if is_s:
                            pass
                        else:
                            memset("pool", halo[:, :, c], 0.0, [r_halo[c]])
                    cp("pool", ae[:, 0:2], halo[:, :, c], [r_halo[c]], [rae])
                    cp("act", ae[:, 2:2 + nb], pa[:, 0:nb], [rpa], [rae])
                    cp("pool", halo[:, :, c], ae[:, nb:nb + 2], [rae], [r_halo[c]])
                    cv, rcv = f32a.next()
                    ts("dve", cv[:, 0:nb], ae[:, 0:nb], convw[:, c, 0:1], convb[:, c:c + 1], ALU.mult, ALU.add, [rae, r_lp], [rcv])
                    stt(cv[:, 0:nb], ae[:, 1:1 + nb], convw[:, c, 1:2], cv[:, 0:nb], ALU.mult, ALU.add, [rae, rcv, r_lp], [rcv])
                    stt(cv[:, 0:nb], ae[:, 2:2 + nb], convw[:, c, 2:3], cv[:, 0:nb], ALU.mult, ALU.add, [rae, rcv, r_lp], [rcv])
                    act(cv[:, 0:nb], cv[:, 0:nb], AF.Silu, [rcv], [rcv])
                    tt("dve", SA[:, ci, 0:nb], cv[:, 0:nb], pv_[:, 0:nb], ALU.mult, [rcv, rpv_], [r_SA[ci]])
                if fh == 0:
                    for dh in range(2):
                        accs = {}
                        for gi in range(3):
                            wdn = W(l, "dn%d_%d_%d" % (fh, dh, gi))
                            wv, rw, _ = wdn
                            ncn = 4 if gi < 2 else 3
                            for (t, nt) in tiles:
                                if gi == 0:
                                    accs[t] = psB.next()
                                pb, rpb = accs[t]
                                mm([(pb[:nt, :], SA[:, gi * 4 + cc, t * 128:t * 128 + nt], wv[:, cc, :],
                                     gi == 0 and cc == 0, gi == 2 and cc == ncn - 1) for cc in range(ncn)],
                                   [rw] + [r_SA[gi * 4 + cc] for cc in range(ncn)], [rpb])
                                if gi == 2:
                                    tt("dve", Xb[:nt, t, dh * 512:(dh + 1) * 512], Xb[:nt, t, dh * 512:(dh + 1) * 512], pb[:nt, :],
                                       ALU.add, [rpb, r_Xb[t]], [r_Xb[t]])
                            Wc(wdn)
                else:
                    wdns = {}
                    for dh in range(2):
                        for gi in range(3):
                            wdns[(dh, gi)] = W(l, "dn%d_%d_%d" % (fh, dh, gi))
                    for (t, nt) in tiles:
                        for dh in range(2):
                            pb, rpb = psB.next()
                            items = []
                            rws = []
                            for gi in range(3):
                                wv, rw, _ = wdns[(dh, gi)]
                                rws.append(rw)
                                ncn = 4 if gi < 2 else 3
                                for cc in range(ncn):
                                    items.append((pb[:nt, :], SA[:, gi * 4 + cc, t * 128:t * 128 + nt], wv[:, cc, :],
                                                  gi == 0 and cc == 0, gi == 2 and cc == ncn - 1))
                            mm(items, rws + [r_SA[i_] for i_ in range(11)], [rpb])
                            tt("dve", Xb[:nt, t, dh * 512:(dh + 1) * 512], Xb[:nt, t, dh * 512:(dh + 1) * 512], pb[:nt, :],
                               ALU.add, [rpb, r_Xb[t]], [r_Xb[t]])
                        finalize_tile(t, nt)
                    Wc(*wdns.values())
            if is_s or j == sq["nblk"] - 1:
                pb, rpb = psA.next()
                mm([(pb[0:2 * NFC, 0:128], halo[:, :, :].rearrange("p j c -> p (j c)"), identf[:, :], True, True)],
                   r_halo + [r_const], [rpb])
                sg_, rsg = stg.next()
                cp("act", sg_[0:2 * NFC, 0:128], pb[0:2 * NFC, 0:128], [rpb], [rsg])
                ocv = s_conv if is_s else p_conv
                bi = 0 if is_s else si
                for jj in range(2):
                    fin.append(dma(ocv[l, bi, jj].rearrange("(c p) -> c p", p=128), sg_[jj * NFC:(jj + 1) * NFC, 0:128], r=[rsg]))
            stage("ffn")

        def sample_past(l):
            ufence(PH_QKV)
            for pbk in range(PT_S // 4):
                ukv_w = W(l, "ukv")
                ukk_w = W(l, "ukk")
                tl = [(t, 128) for t in range(4)]
                for (t, nt) in tl:
                    kt = pbk * 4 + t
                    r0 = kt * 128
                    cb, rcb = b16a.next()
                    dma(cb[:, 0:KVL], c_ckv[l, 0, r0:r0 + 128, :], w=[rcb], q="pool")
                    dma(krpad_b[kt % 2][:, 64:96], c_kr[l, 0, r0:r0 + 128, :], w=[r_krpad_b[kt % 2]], q="pool")
                    db, rdb = b16a.next()
                    dma(db[:, :], c_dk[l, 0, r0:r0 + 128, :], w=[rdb], q="pool")
                    dma(Vd[:, kt, :, 0:64], c_dv[l, 0, r0:r0 + 128, :].rearrange("p (h c) -> p h c", c=64), w=[r_Vd[kt]], q="pool")
                    mla_append_tile(l, t, nt, kt, cb, rcb, ukv_w)
                    dk_append_tile(l, t, nt, kt, db, rdb)
                knope_block(l, tl, pbk * 4, 512, ukk_w)
                Wc(ukv_w, ukk_w)
            sg_, rsg = stg.next()
            for jj in range(2):
                dma(sg_[jj * NFC:(jj + 1) * NFC, 0:128], st_cv[l, 0, jj].rearrange("(c p) -> c p", p=128), w=[rsg])
            pb, rpb = psA.next()
            mm([(pb[:, 0:2 * NFC], sg_[0:2 * NFC, 0:128], identf[0:2 * NFC, 0:2 * NFC], True, True)], [rsg, r_const], [rpb])
            cp("act", halo[:, :, :].rearrange("p j c -> p (j c)"), pb[:, 0:2 * NFC], [rpb], r_halo)

        try:
            stage("setup")
            for l in range(DEPTH):
                load_layer_params(l)
                stage("params")
                nxt = conv_by_layer.get(l + 1, [])
                nblocks_p = max(1, NP * NBLK)
                cast_state["queue"] = list(nxt)
                cast_state["per_block"] = -(-len(nxt) // max(1, nblocks_p // 2))
                for sq in seqs:
                    if sq["kind"] == "s":
                        sample_past(l)
                    for j in range(sq["nblk"]):
                        do_block(l, sq, j)
                        cast_some()
                while cast_state["queue"]:
                    cast_some()
            assert wstate["used"] == len(stream)
        except StopBuild:
            print("build stopped at", STOP)
        print("sbuf bytes remaining:", nc.sbuf_bytes_remaining, "ops:", {e: len(P.ops[e]) for e in ENGS})
        P.emit(fin)
    return nc


_OUT_ORDER = ["y_prompt", "y_sample", "p_ckv", "p_krope", "p_dk", "p_dv", "p_hgrn", "p_conv",
              "s_ckv", "s_krope", "s_dk", "s_dv", "s_hgrn", "s_conv"]


def kernel(**inputs):
    return run(inputs, 8)


def run(inputs, NCORE, dbg=(), sample=True):
    inp = {k: np.asarray(v) for k, v in inputs.items()}
    B, T = inp["x_prompt"].shape[0], inp["x_prompt"].shape[1]
    SB, TS = inp["x_sample"].shape[0], inp["x_sample"].shape[1]
    DEPTH = inp["w_in"].shape[0]
    PAST = inp["cache_mla_ckv"].shape[2]
    NP = B // NCORE
    assert SB == NCORE
    nc = build(NP=NP, T=T, DEPTH=DEPTH, PAST=PAST, TS=TS, SAMPLE=sample, dbg=dbg)
    consts = _consts(max(T // 128, PAST // 128 + 1) if sample else T // 128)
    weights = ["norm_mix_g", "w_in", "mla_q_norm_g", "mla_w_uq", "mla_kv_norm_g", "mla_w_ukv", "hgrn_lb_logits",
               "hgrn_norm_g", "diff_lambda", "diff_norm_g", "w_branch", "w_out", "norm_ffn_g", "ffn_w_up",
               "ffn_conv_w", "ffn_conv_b", "ffn_w_down", "norm_final_g"]
    in_maps = []
    for c in range(NCORE):
        m = {w: np.ascontiguousarray(inp[w], dtype=np.float32) for w in weights}
        m.update(consts)
        m["x_prompt"] = np.ascontiguousarray(inp["x_prompt"][c * NP:(c + 1) * NP])
        m["x_sample"] = np.ascontiguousarray(inp["x_sample"][c:c + 1])
        m["cache_mla_ckv"] = np.ascontiguousarray(inp["cache_mla_ckv"][:, c:c + 1])
        m["cache_mla_krope"] = np.ascontiguousarray(inp["cache_mla_krope"][:, c:c + 1])
        m["cache_diff_k"] = np.ascontiguousarray(inp["cache_diff_k"][:, c:c + 1]).reshape(DEPTH, 1, PAST, 512)
        m["cache_diff_v"] = np.ascontiguousarray(inp["cache_diff_v"][:, c:c + 1]).reshape(DEPTH, 1, PAST, 512)
        m["state_hgrn"] = np.ascontiguousarray(inp["state_hgrn"][:, c:c + 1])
        m["state_ffn_conv"] = np.ascontiguousarray(inp["state_ffn_conv"][:, c:c + 1])
        in_maps.append(m)
    if not sample:
        for m in in_maps:
            for k in ("x_sample", "cache_mla_ckv", "cache_mla_krope", "cache_diff_k", "cache_diff_v", "state_hgrn", "state_ffn_conv"):
                m.pop(k)
    import os
    if os.environ.get("KTRACE"):
        res = run_bass_kernel_spmd(nc, in_maps, core_ids=list(range(NCORE)), trace=True)
        print("EXEC_TIME_NS", res.exec_time_ns)
    else:
        res = run_bass_kernel_spmd(nc, in_maps, core_ids=list(range(NCORE)))
    rs = res.results
    if dbg:
        return rs
    outs = []
    for name in _OUT_ORDER:
        ax = 0 if name in ("y_prompt", "y_sample") else 1
        a = np.concatenate([np.asarray(r[name]) for r in rs], axis=ax).astype(np.float32)
        if name in ("p_dk", "p_dv", "s_dk", "s_dv"):
            a = a.reshape(a.shape[0], a.shape[1], a.shape[2], 8, 64)
        outs.append(a)
    return tuple(outs)
```

```python
import contextlib
import math

import ml_dtypes
import numpy as np

import concourse.bass as bass
import concourse.mybir as mybir
from concourse.bass_utils import run_bass_kernel_spmd

F32 = mybir.dt.float32
BF16 = mybir.dt.bfloat16
AF = mybir.ActivationFunctionType
ALU = mybir.AluOpType
AX = mybir.AxisListType

D = 1024
CHUNK = 64
EPS = 1e-6
F_MIN = 1e-12
MH, MN, MR, MV = 8, 64, 32, 64
QL, KVL = 384, 256
THETA = 10000.0
MLA_SCALE = (MN + MR) ** -0.5
HH, HDK, HDV = 8, 64, 64
DH, DDH, DDV = 8, 32, 64
DF_SCALE = DDH ** -0.5
DFF = 2816
NFC = DFF // 128
IN_COLS = 7328
OFF_Q, OFF_KV, OFF_KR, OFF_HQ, OFF_HF, OFF_HI, OFF_HG, OFF_DQ, OFF_DK, OFF_DV, OFF_G = (
    0, 384, 640, 672, 1184, 1696, 2208, 2720, 3232, 3744, 4256)

ENGS = ("pe", "act", "dve", "pool", "sp")
N_DMA_SEMS = 10


class Res:
    __slots__ = ("name", "writer", "readers", "excl")

    def __init__(self, name, excl=False):
        self.name = name
        self.writer = None
        self.readers = []
        self.excl = excl


class Op:
    __slots__ = ("eng", "fn", "deps", "signal", "sem", "count", "is_dma")

    def __init__(self, eng, fn, is_dma):
        self.eng = eng
        self.fn = fn
        self.deps = []
        self.signal = False
        self.sem = None
        self.count = 0
        self.is_dma = is_dma


class Prog:
    def __init__(self, nc):
        self.nc = nc
        self.ops = {e: [] for e in ENGS}
        self.dma_rr = {e: 0 for e in ENGS}
        self.dma_last = {}

    def op(self, eng, fn, reads=(), writes=(), dma=False):
        o = Op(eng, fn, dma)
        deps = {}
        if any(r.excl for r in reads):
            writes = list(writes) + [r for r in reads if r.excl]
            reads = [r for r in reads if not r.excl]
        for r in reads:
            if r.writer is not None:
                deps[id(r.writer)] = r.writer
        for w in writes:
            if w.writer is not None:
                deps[id(w.writer)] = w.writer
            for rd in w.readers:
                deps[id(rd)] = rd
        if dma:
            slot = self.dma_rr[eng] % N_DMA_SEMS
            self.dma_rr[eng] += 1
            prev = self.dma_last.get((eng, slot))
            if prev is not None:
                deps[id(prev)] = prev
            self.dma_last[(eng, slot)] = o
            o.sem = (eng, slot)
            o.signal = True
        for d in deps.values():
            if d is o:
                continue
            if d.eng == "pe" and eng == "pe" and not d.is_dma and not dma:
                continue
            o.deps.append(d)
            d.signal = True
        for w in writes:
            w.writer = o
            w.readers = []
        for r in reads:
            r.readers.append(o)
        self.ops[eng].append(o)
        return o

    def emit(self, final_ops):
        nc = self.nc
        for o in final_ops:
            o.signal = True
        with contextlib.ExitStack() as st:
            sems = {}
            for e in ENGS:
                sems[e] = st.enter_context(nc.semaphore("s_" + e))
            for e in ("sp", "pool"):
                for k in range(N_DMA_SEMS):
                    sems[(e, k)] = st.enter_context(nc.semaphore("d_%s_%d" % (e, k)))
            cnt = {}
            for e in ENGS:
                for o in self.ops[e]:
                    if not o.signal:
                        continue
                    key = o.sem if o.is_dma else e
                    o.sem = key
                    cnt[key] = cnt.get(key, 0) + (16 if o.is_dma else 1)
                    o.count = cnt[key]
            block = st.enter_context(nc.Block())

            def run(e, eng):
                waited = {}
                for o in self.ops[e]:
                    for d in o.deps:
                        if waited.get(d.sem, 0) < d.count:
                            eng.wait_ge(sems[d.sem], d.count)
                            waited[d.sem] = d.count
                    ins = o.fn(eng)
                    if o.signal:
                        ins.then_inc(sems[o.sem], 16 if o.is_dma else 1)
                if e == "sp":
                    for key, total in cnt.items():
                        if isinstance(key, tuple) and waited.get(key, 0) < total:
                            eng.wait_ge(sems[key], total)
                            waited[key] = total

            @block.tensor
            def _(eng):
                run("pe", eng)

            @block.scalar
            def _(eng):
                run("act", eng)

            @block.vector
            def _(eng):
                run("dve", eng)

            @block.gpsimd
            def _(eng):
                run("pool", eng)

            @block.sync
            def _(eng):
                run("sp", eng)


class Ring:
    def __init__(self, tiles, name, excl=False):
        self.tiles = tiles
        self.res = [Res("%s%d" % (name, i), excl) for i in range(len(tiles))]
        self.i = 0

    def next(self):
        k = self.i % len(self.tiles)
        self.i += 1
        return self.tiles[k], self.res[k]


def _consts(npt):
    s = np.arange(128)[:, None]
    t = np.arange(128)[None, :]
    tri = (s <= t).astype(np.float32)
    tribd = ((s <= t) & (s // 64 == t // 64)).astype(np.float32)
    dc = ((s > t) & (s < 64)).astype(np.float32)[:, :64]
    dmat = (s > t).astype(np.float32)
    hgm = np.concatenate([tribd, -tribd, tri, dc, dmat], axis=1)
    half = MR // 2
    inv = THETA ** (-np.arange(half, dtype=np.float32) / half)
    pos = (np.arange(npt)[None, :] * 128 + np.arange(128)[:, None]).astype(np.float32)
    ang = pos[:, :, None] * inv[None, None, :]
    m01 = np.zeros((128, 4), np.float32)
    for v in range(4):
        m01[:, v] = (np.arange(128) // 32 == v)
    return {
        "c_identb": np.eye(128).astype(ml_dtypes.bfloat16),
        "c_identf": np.eye(128).astype(np.float32),
        "c_hgm": hgm.astype(ml_dtypes.bfloat16),
        "c_maskbd": tribd.astype(ml_dtypes.bfloat16),
        "c_cos": np.cos(ang).astype(np.float32),
        "c_sin": np.sin(ang).astype(np.float32),
        "c_m01": m01,
    }


def build(NP=4, T=2048, DEPTH=2, PAST=2048, TS=16, SAMPLE=True, dbg=()):
    nc = bass.Bass("TRN2", target_bir_lowering=False)
    P = Prog(nc)
    NBLK = T // 512
    KT_P = T // 128
    PT_S = PAST // 128
    KT_S = PT_S + 1
    KTMAX = max(KT_P, KT_S if SAMPLE else 0)
    KMAX = max(T, PAST + TS if SAMPLE else 0)
    NPT = KTMAX
    NSEQ = NP + (1 if SAMPLE else 0)

    def din(name, shape, dt=F32):
        return nc.dram_tensor(name, list(shape), dt, kind="ExternalInput").ap()

    def dout(name, shape):
        return nc.dram_tensor(name, list(shape), F32, kind="ExternalOutput").ap()

    def dscr(name, shape, dt):
        return nc.dram_tensor(name, list(shape), dt).ap()

    x_prompt = din("x_prompt", [NP, T, D])
    if SAMPLE:
        x_sample = din("x_sample", [1, TS, D])
        c_ckv = din("cache_mla_ckv", [DEPTH, 1, PAST, KVL])
        c_kr = din("cache_mla_krope", [DEPTH, 1, PAST, MR])
        c_dk = din("cache_diff_k", [DEPTH, 1, PAST, DH * 2 * DDH])
        c_dv = din("cache_diff_v", [DEPTH, 1, PAST, DH * DDV])
        st_hg = din("state_hgrn", [DEPTH, 1, HH, HDK, HDV])
        st_cv = din("state_ffn_conv", [DEPTH, 1, 2, DFF])
    norm_mix_g = din("norm_mix_g", [DEPTH, D])
    w_in = din("w_in", [DEPTH, D, IN_COLS])
    mla_q_norm_g = din("mla_q_norm_g", [DEPTH, QL])
    mla_w_uq = din("mla_w_uq", [DEPTH, QL, MH * (MN + MR)])
    mla_kv_norm_g = din("mla_kv_norm_g", [DEPTH, KVL])
    mla_w_ukv = din("mla_w_ukv", [DEPTH, KVL, MH * (MN + MV)])
    hgrn_lb_logits = din("hgrn_lb_logits", [DEPTH, HH * HDK])
    hgrn_norm_g = din("hgrn_norm_g", [DEPTH, HDV])
    diff_lambda = din("diff_lambda", [DEPTH, 4, DDH])
    diff_norm_g = din("diff_norm_g", [DEPTH, DDV])
    w_branch = din("w_branch", [DEPTH, 3, 512, D])
    w_out = din("w_out", [DEPTH, D, D])
    norm_ffn_g = din("norm_ffn_g", [DEPTH, D])
    ffn_w_up = din("ffn_w_up", [DEPTH, D, 2 * DFF])
    ffn_conv_w = din("ffn_conv_w", [DEPTH, 3, DFF])
    ffn_conv_b = din("ffn_conv_b", [DEPTH, DFF])
    ffn_w_down = din("ffn_w_down", [DEPTH, DFF, D])
    norm_final_g = din("norm_final_g", [D])
    c_identb = din("c_identb", [128, 128], BF16)
    c_identf = din("c_identf", [128, 128])
    c_hgm = din("c_hgm", [128, 576], BF16)
    c_maskbd = din("c_maskbd", [128, 128], BF16)
    c_cos = din("c_cos", [128, NPT, 16])
    c_sin = din("c_sin", [128, NPT, 16])
    c_m01 = din("c_m01", [128, 4])

    y_prompt = dout("y_prompt", [NP, T, D])
    p_ckv = dout("p_ckv", [DEPTH, NP, T, KVL])
    p_krope = dout("p_krope", [DEPTH, NP, T, MR])
    p_dk = dout("p_dk", [DEPTH, NP, T, 512])
    p_dv = dout("p_dv", [DEPTH, NP, T, 512])
    p_hgrn = dout("p_hgrn", [DEPTH, NP, HH, HDK, HDV])
    p_conv = dout("p_conv", [DEPTH, NP, 2, DFF])
    if SAMPLE:
        y_sample = dout("y_sample", [1, TS, D])
        s_ckv = dout("s_ckv", [DEPTH, 1, TS, KVL])
        s_krope = dout("s_krope", [DEPTH, 1, TS, MR])
        s_dk = dout("s_dk", [DEPTH, 1, TS, 512])
        s_dv = dout("s_dv", [DEPTH, 1, TS, 512])
        s_hgrn = dout("s_hgrn", [DEPTH, 1, HH, HDK, HDV])
        s_conv = dout("s_conv", [DEPTH, 1, 2, DFF])
    dbg_out = {}
    for name, shape in dbg:
        dbg_out[name] = dout("dbg_" + name, shape)

    xs = dscr("xs", [NSEQ, T, D], F32)
    bcd = dscr("bcd", [4, 512], F32)
    bcd_ring = Ring([bcd[i] for i in range(4)], "bcd")
    r_xs = [[Res("xs%d_%d" % (s, j)) for j in range(NBLK)] for s in range(NSEQ)]

    pieces = {}
    conv_list = []

    def piece(l, name, shape, srcs):
        scr = dscr("w%d_%s" % (l, name), [128] + list(shape), BF16)
        r = Res("w%d_%s" % (l, name))
        pieces[(l, name)] = (scr, shape, r)
        for dv, src in srcs:
            conv_list.append((dv(scr), src, r))

    for l in range(DEPTH):
        wi = w_in[l].rearrange("(kc p) n -> p kc n", p=128)

        def colpiece(name, c0, n, l=l, wi=wi):
            piece(l, name, [8, n], [(lambda s: s[:, :, :], wi[:, :, c0:c0 + n])])

        colpiece("q0", 0, 256)
        colpiece("q1", 256, 128)
        colpiece("kv", OFF_KV, 256)
        colpiece("kr", OFF_KR, 32)
        for gname, off in (("hq", OFF_HQ), ("hf", OFF_HF), ("hi", OFF_HI), ("hg", OFF_HG),
                           ("dq", OFF_DQ), ("dk", OFF_DK), ("dv", OFF_DV)):
            colpiece(gname + "0", off, 256)
            colpiece(gname + "1", off + 256, 256)
        wb = w_branch[l].rearrange("n (kc p) d -> p kc n d", p=128)
        for d in range(8):
            g0 = OFF_G + d * 128
            piece(l, "gA%d" % d, [8, 2, 128], [
                (lambda s: s[:, :, 0, :], wi[:, :, g0:g0 + 128]),
                (lambda s: s[:, :, 1, :], wi[:, :, g0 + 1024:g0 + 1024 + 128])])
            piece(l, "gB%d" % d, [8, 128], [(lambda s: s[:, :, :], wi[:, :, g0 + 2048:g0 + 2048 + 128])])
            piece(l, "br%d" % d, [4, 3, 128], [
                (lambda s, n=n: s[:, :, n, :], wb[:, :, n, d * 128:(d + 1) * 128]) for n in range(3)])
        uq = mla_w_uq[l].rearrange("(kc p) n -> p kc n", p=128)
        piece(l, "uq0", [3, 384], [(lambda s: s[:, :, :], uq[:, :, 0:384])])
        piece(l, "uq1", [3, 384], [(lambda s: s[:, :, :], uq[:, :, 384:768])])
        ukv = mla_w_ukv[l].rearrange("(kc p) (h c) -> p kc h c", p=128, c=128)
        piece(l, "ukk", [2, 8, 64], [(lambda s, c=c: s[:, c, :, :], ukv[:, c, :, 0:64]) for c in range(2)])
        piece(l, "ukv", [2, 8, 64], [(lambda s, c=c: s[:, c, :, :], ukv[:, c, :, 64:128]) for c in range(2)])
        wo = w_out[l].rearrange("(kc p) n -> p kc n", p=128)
        for i in range(4):
            piece(l, "wo%d" % i, [8, 256], [(lambda s: s[:, :, :], wo[:, :, i * 256:(i + 1) * 256])])
        wu = ffn_w_up[l].rearrange("(kc p) n -> p kc n", p=128)
        for c in range(NFC):
            piece(l, "up%d" % c, [8, 2, 128], [
                (lambda s: s[:, :, 0, :], wu[:, :, c * 128:(c + 1) * 128]),
                (lambda s: s[:, :, 1, :], wu[:, :, DFF + c * 128:DFF + (c + 1) * 128])])
        wd = ffn_w_down[l].rearrange("(c p) n -> p c n", p=128)
        for fh in range(2):
            for dh in range(2):
                for gi in range(3):
                    c0 = fh * 11 + gi * 4
                    ncn = min(4, fh * 11 + 11 - c0)
                    piece(l, "dn%d_%d_%d" % (fh, dh, gi), [ncn, 512], [
                        (lambda s: s[:, :, :], wd[:, c0:c0 + ncn, dh * 512:(dh + 1) * 512])])

    def block_piece_order():
        o = ["q0", "q1", "uq0", "uq1", "ukv", "ukk", "kv", "kr",
             "hq0", "hq1", "hf0", "hf1", "hi0", "hi1", "hg0", "hg1",
             "dq0", "dq1", "dk0", "dk1", "dv0", "dv1"]
        for d in range(8):
            o += ["gA%d" % d, "gB%d" % d, "br%d" % d]
        o += ["wo0", "wo1", "wo2", "wo3"]
        for fh in range(2):
            o += ["up%d" % c for c in range(fh * 11, fh * 11 + 11)]
            for dh in range(2):
                o += ["dn%d_%d_%d" % (fh, dh, gi) for gi in range(3)]
        return o

    past_piece_order = ["ukv", "ukk"]

    seqs = [dict(kind="p", idx=s, nblk=NBLK) for s in range(NP)]
    if SAMPLE:
        seqs.append(dict(kind="s", idx=NP, nblk=1))
    stream = []
    for l in range(DEPTH):
        for sq in seqs:
            if sq["kind"] == "s":
                for pb in range(PT_S // 4):
                    stream += [(l, n) for n in past_piece_order]
            for j in range(sq["nblk"]):
                stream += [(l, n) for n in block_piece_order()]

    with contextlib.ExitStack() as st:
        def sb(name, shape, dt=F32):
            return st.enter_context(nc.sbuf_tensor(name, list(shape), dt))

        identb = sb("identb", [128, 128], BF16)
        identf = sb("identf", [128, 128])
        hgm = sb("hgm", [128, 576], BF16)
        maskbd = sb("maskbd", [128, 128], BF16)
        cosT = sb("cosT", [128, NPT, 16])
        sinT = sb("sinT", [128, NPT, 16])
        m01 = sb("m01", [128, 4])
        ones_f = sb("ones_f", [128, 64])
        ones_b = sb("ones_b", [128, 64], BF16)
        eps_t = sb("eps_t", [128, 1])
        r_const = Res("const")
        gmix = sb("gmix", [128, 8])
        gffn = sb("gffn", [128, 8])
        gq = sb("gq", [128, 3])
        gkv_bc = sb("gkv_bc", [128, KVL])
        lb_bc = sb("lb_bc", [128, 512])
        ghg2 = sb("ghg2", [128, 1])
        gdf = sb("gdf", [128, 1])
        dlam = sb("dlam", [128, 4, DDH])
        lam2 = sb("lam2", [128, 4])
        convw = sb("convw", [128, NFC, 3])
        convb = sb("convb", [128, NFC])
        r_lp = Res("layerparams")
        kTm = sb("kTm", [128, MH, KMAX], BF16)
        Vm = sb("Vm", [128, KTMAX + 1, MH, 65], BF16)
        kTd = sb("kTd", [128, 4, KMAX], BF16)
        Vd = sb("Vd", [128, KTMAX + 1, DH, 65], BF16)
        r_kTm = [Res("kTm%d" % i) for i in range(KTMAX)]
        r_Vm = [Res("Vm%d" % i) for i in range(KTMAX)]
        r_kTd = [Res("kTd%d" % i) for i in range(KTMAX)]
        r_Vd = [Res("Vd%d" % i) for i in range(KTMAX)]
        S32 = sb("S32", [128, 4, 64])
        Sbf = sb("Sbf", [128, 4, 64], BF16)
        r_S32, r_Sbf = Res("S32"), Res("Sbf")
        halo = sb("halo", [128, 2, NFC])
        r_halo = [Res("halo%d" % c) for c in range(NFC)]
        Xb = sb("Xb", [128, 4, D])
        r_Xb = [Res("Xb%d" % t) for t in range(4)]
        hT = sb("hT", [128, 8, 512], BF16)
        r_hT = [Res("hT%d" % t) for t in range(4)]
        NSLOT = 8
        wring = Ring([sb("wr%d" % i, [128, 2048], BF16) for i in range(NSLOT)], "wr")
        SA = sb("SA", [128, 20, 512], BF16)
        r_SA = [Res("SA%d" % i) for i in range(20)]
        dqT0 = sb("dqT0", [128, 4, 512], BF16)
        dqT1 = sb("dqT1", [128, 4, 512], BF16)
        r_dqT = [Res("dqT%d" % c) for c in range(4)]
        r_dqv = Res("dqv")
        Uf = Xb[:, :, :].rearrange("p t d -> p (t d)")
        Ub = Uf.bitcast(BF16)

        def uview(off, shape, dt):
            n = int(np.prod(shape))
            if dt == BF16:
                v = Ub[:, off // 2:off // 2 + n]
            else:
                v = Uf[:, off // 4:off // 4 + n]
            if len(shape) == 2:
                return v.rearrange("p (a b) -> p a b", b=shape[1])
            return v

        qlatT = uview(0, [3, 512], BF16)
        r_qlatT = [Res("qlatT%d" % t) for t in range(4)]
        qtm_b = [uview(3072, [8, 96], BF16), uview(7680, [8, 96], BF16)]
        r_qtm_b = [Res("qtm0"), Res("qtm1")]
        qr_b = [uview(4608, [8, 32], F32), uview(9216, [8, 32], F32)]
        r_qr_b = [Res("qr0"), Res("qr1")]
        ckvT = uview(5632, [2, 512], BF16)
        r_ckvT = [Res("ckvT%d" % t) for t in range(4)]
        EX = uview(0, [4, 448], BF16)
        r_EX = Res("EX")
        hgp = Ring([uview(3584 + 1024 * i, [4, 128], BF16) for i in range(4)], "hgp")
        att_sb = uview(7680, [8, 128], BF16)
        r_att = Res("att")
        Pr = Ring([uview(1024 * i, [512], BF16) for i in range(3)], "Pr")
        rcp = uview(3072, [2, 512], F32)
        rcph = uview(7168, [8, 512], BF16)
        r_rcp = [Res("rcp0"), Res("rcp1")]
        r_rcph = [Res("rcph%d" % i) for i in range(4)]
        PH_QKV = r_qlatT + r_qtm_b + r_qr_b + r_ckvT
        PH_HG = [r_EX, r_att] + hgp.res
        PH_ATT = Pr.res + r_rcp + r_rcph
        fdummy = sb("fdummy", [128, 1])
        r_fd = Res("fdummy")

        def ufence(new):
            allr = r_Xb + PH_QKV + PH_HG + PH_ATT
            P.op("pool", lambda e: e.memset(fdummy[:, 0:1], 0.0), (), allr + [r_fd])
        stg = Ring([sb("stg%d" % i, [128, 512]) for i in range(3)], "stg")
        hb = Ring([sb("hb%d" % i, [128, D], BF16) for i in range(2)], "hb")
        sm = Ring([sb("sm%d" % i, [128, 16]) for i in range(6)], "sm")
        f32a = Ring([sb("fa%d" % i, [128, 512]) for i in range(4)], "fa")
        b16a = Ring([sb("ba%d" % i, [128, 512], BF16) for i in range(5)], "ba")
        krpad_b = [sb("krpad0", [128, 96], BF16), sb("krpad1", [128, 96], BF16)]
        r_krpad_b = [Res("krpad0"), Res("krpad1")]
        aext = Ring([dq_[:, :, :].rearrange("p c n -> p (c n)").bitcast(F32)[:, 0:514] for dq_ in (dqT0, dqT1)], "aext")

        pbank = [st.enter_context(nc.psum_tensor("pb%d" % i, [128, 512], F32)) for i in range(8)]
        psA = Ring(pbank[0:4], "psA", excl=True)
        psB = Ring(pbank[4:8], "psB", excl=True)

        fin = []

        def dma(out, in_, r=(), w=(), q="sp", slow=False):
            if slow:
                return P.op(q, lambda e: e.dma_start(out=out, in_=in_, allow_slow_non_contiguous=True), r, w, dma=True)
            return P.op(q, lambda e: e.dma_start(out=out, in_=in_), r, w, dma=True)

        def act(out, in_, func, r, w, **kw):
            return P.op("act", lambda e: e.activation(out=out, in_=in_, func=func, **kw), r, w)

        def tt(eng, out, in0, in1, op, r, w):
            return P.op(eng, lambda e: e.tensor_tensor(out=out, in0=in0, in1=in1, op=op), r, w)

        def ts(eng, out, in0, s1, s2, op0, op1, r, w):
            if s2 is None:
                return P.op(eng, lambda e: e.tensor_scalar(out=out, in0=in0, scalar1=s1, scalar2=None, op0=op0), r, w)
            return P.op(eng, lambda e: e.tensor_scalar(out=out, in0=in0, scalar1=s1, scalar2=s2, op0=op0, op1=op1), r, w)

        def stt(out, in0, scalar, in1, op0, op1, r, w):
            return P.op("dve", lambda e: e.scalar_tensor_tensor(out=out, in0=in0, scalar=scalar, in1=in1, op0=op0, op1=op1), r, w)

        def cp(eng, out, in_, r, w):
            if eng == "act":
                return P.op("act", lambda e: e.activation(out=out, in_=in_, func=AF.Copy), r, w)
            return P.op(eng, lambda e: e.tensor_copy(out=out, in_=in_), r, w)

        def memset(eng, ap, val, w):
            return P.op(eng, lambda e: e.memset(ap, val), (), w)

        def mm(items, r, w):
            def fn(e):
                ins = None
                for (o, a, b, s0, s1) in items:
                    ins = e.matmul(o, a, b, start=s0, stop=s1)
                return ins
            return P.op("pe", fn, r, w)

        def trs(items, r, w):
            def fn(e):
                ins = None
                for (o, a, idn) in items:
                    ins = e.transpose(o, a, idn)
                return ins
            return P.op("pe", fn, r, w)

        def dbgdump(name, ap, r):
            if name in dbg_out:
                fin.append(dma(dbg_out[name], ap, r=r))

        import os
        STOP = os.environ.get("KSTOP", "")
        SKIP = set(os.environ.get("KSKIP", "").split(","))

        class StopBuild(Exception):
            pass

        def stage(name):
            if STOP == name:
                raise StopBuild()

        for dst, src in ((identb, c_identb), (identf, c_identf), (hgm, c_hgm), (maskbd, c_maskbd),
                         (m01, c_m01)):
            dma(dst[:], src[:, :], w=[r_const])
        dma(cosT[:], c_cos[:, :, :], w=[r_const])
        dma(sinT[:], c_sin[:, :, :], w=[r_const])
        memset("pool", ones_f[:], 1.0, [r_const])
        memset("pool", ones_b[:], 1.0, [r_const])
        memset("pool", eps_t[:], EPS, [r_const])
        for i_ in range(2):
            memset("pool", krpad_b[i_][:], 0.0, [r_krpad_b[i_]])
        for kt in range(KTMAX):
            memset("pool", Vm[:, kt, :, :], 0.0, [r_Vm[kt]])
            memset("pool", Vd[:, kt, :, :], 0.0, [r_Vd[kt]])
            memset("pool", Vm[:, kt, :, 64:65], 1.0, [r_Vm[kt]])
            memset("pool", Vd[:, kt, :, 64:65], 1.0, [r_Vd[kt]])
        memset("pool", Vm[:, KTMAX, :, :], 0.0, [r_const])
        memset("pool", Vd[:, KTMAX, :, :], 0.0, [r_const])
        memset("pool", kTm[96:128, :, :], 0.0, [r_const])
        Vm_flat = Vm[:, :, :, :].rearrange("p k h c -> p (k h c)")
        Vd_flat = Vd[:, :, :, :].rearrange("p k h c -> p (k h c)")

        conv_by_layer = {}
        for dst, src, r in conv_list:
            conv_by_layer.setdefault(int(r.name[1:r.name.index("_")]), []).append((dst, src, r))
        early_names = set(block_piece_order()[:22])
        late0 = []
        for dst, src, r in conv_by_layer.get(0, []):
            if r.name[r.name.index("_") + 1:] in early_names:
                dma(dst, src, w=[r], q="pool")
            else:
                late0.append((dst, src, r))
        cast_state = {"queue": [], "per_block": 0, "late0": late0}

        def cast_some():
            for _ in range(cast_state["per_block"]):
                if cast_state["queue"]:
                    dst, src, r = cast_state["queue"].pop(0)
                    dma(dst, src, w=[r], q="pool")

        wstate = dict(issued=0, used=0, slots={}, closed=set())

        def w_issue():
            while wstate["issued"] < len(stream) and (
                    wstate["issued"] < NSLOT or (wstate["issued"] - NSLOT) in wstate["closed"]):
                i = wstate["issued"]
                l, name = stream[i]
                scr, shape, r = pieces[(l, name)]
                tile_, rs = wring.next()
                nel = int(np.prod(shape))
                view = tile_[:, 0:nel]
                if len(shape) == 2:
                    src = scr[:, :, :].rearrange("p a b -> p (a b)")
                    dma(view, src, r=[r], w=[rs])
                    view = view.rearrange("p (a b) -> p a b", b=shape[1])
                else:
                    src = scr[:, :, :, :].rearrange("p a b c -> p (a b c)")
                    dma(view, src, r=[r], w=[rs])
                    view = view.rearrange("p (a b c) -> p a b c", b=shape[1], c=shape[2])
                wstate["slots"][i] = (view, rs)
                wstate["issued"] += 1

        def W(l, name):
            i = wstate["used"]
            assert stream[i] == (l, name), (i, stream[i], (l, name))
            w_issue()
            assert wstate["issued"] > i, "weight ring deadlock at piece %d %s" % (i, name)
            wstate["used"] += 1
            v, r = wstate["slots"].pop(i)
            return (v, r, i)

        def Wc(*ws):
            for w_ in ws:
                wstate["closed"].add(w_[2])
            w_issue()

        def load_layer_params(l):
            wl = [r_lp]
            dma(gmix[:], norm_mix_g[l].rearrange("(kc p) -> p kc", p=128), w=wl, slow=True)
            dma(gffn[:], norm_ffn_g[l].rearrange("(kc p) -> p kc", p=128), w=wl, slow=True)
            dma(gq[:], mla_q_norm_g[l].rearrange("(kc p) -> p kc", p=128), w=wl, slow=True)
            dma(gkv_bc[:], mla_kv_norm_g[l].partition_broadcast(128), w=wl)
            dma(ghg2[0:64, :], hgrn_norm_g[l].rearrange("(d o) -> d o", o=1), w=wl)
            dma(ghg2[64:128, :], hgrn_norm_g[l].rearrange("(d o) -> d o", o=1), w=wl)
            dma(dlam[:], diff_lambda[l].rearrange("a c -> (a c)").partition_broadcast(128).rearrange("p (a c) -> p a c", c=DDH), w=wl)
            dma(gdf[0:64, :], diff_norm_g[l].rearrange("(d o) -> d o", o=1), w=wl)
            dma(gdf[64:128, :], diff_norm_g[l].rearrange("(d o) -> d o", o=1), w=wl)
            for jj in range(3):
                dma(convw[:, :, jj], ffn_conv_w[l, jj].rearrange("(c p) -> p c", p=128), w=wl, slow=True)
            dma(convb[:], ffn_conv_b[l].rearrange("(c p) -> p c", p=128), w=wl, slow=True)
            fm, rfm = f32a.next()
            fd, rfd = f32a.next()
            for i in range(DEPTH):
                tg, rtg = stg.next()
                dma(tg[:], hgrn_lb_logits[i].partition_broadcast(128), w=[rtg])
                if i == 0:
                    cp("dve", fm[:], tg[:], [rtg], [rfm])
                else:
                    tt("dve", fm[:], fm[:], tg[:], ALU.max, [rfm, rtg], [rfm])
            memset("pool", lb_bc[:], 0.0, wl)
            for i in range(DEPTH):
                tg, rtg = stg.next()
                dma(tg[:], hgrn_lb_logits[i].partition_broadcast(128), w=[rtg])
                tt("dve", tg[:], tg[:], fm[:], ALU.subtract, [rfm, rtg], [rtg])
                act(tg[:], tg[:], AF.Exp, [rtg], [rtg])
                if i == 0:
                    cp("dve", fd[:], tg[:], [rtg], [rfd])
                else:
                    tt("dve", fd[:], fd[:], tg[:], ALU.add, [rfd, rtg], [rfd])
                if 1 <= i <= l:
                    tt("dve", lb_bc[:], lb_bc[:], tg[:], ALU.add, [rtg] + wl, wl)
            P.op("dve", lambda e: e.reciprocal(out=fd[:], in_=fd[:]), [rfd], [rfd])
            tt("dve", lb_bc[:], lb_bc[:], fd[:], ALU.mult, [rfd] + wl, wl)
            lam_init = 0.8 - 0.6 * math.exp(-0.3 * l)
            tt("dve", dlam[:, 0, :], dlam[:, 0, :], dlam[:, 1, :], ALU.mult, wl, wl)
            tt("dve", dlam[:, 2, :], dlam[:, 2, :], dlam[:, 3, :], ALU.mult, wl, wl)
            P.op("dve", lambda e: e.reduce_sum(out=lam2[:, 0:1], in_=dlam[:, 0, :], axis=AX.X), wl, wl)
            P.op("dve", lambda e: e.reduce_sum(out=lam2[:, 1:2], in_=dlam[:, 2, :], axis=AX.X), wl, wl)
            act(lam2[:, 0:2], lam2[:, 0:2], AF.Exp, wl, wl)
            tt("dve", lam2[:, 2:3], lam2[:, 0:1], lam2[:, 1:2], ALU.subtract, wl, wl)
            ts("dve", lam2[:, 3:4], lam2[:, 2:3], -1.0, -lam_init, ALU.mult, ALU.add, wl, wl)
            ts("dve", gdf[:], gdf[:], 1.0 - lam_init, None, ALU.mult, None, wl, wl)

        def rstd(src, nt, dim, rsrc, junkbuf=None):
            junk, rj = junkbuf if junkbuf is not None else hb.next()
            s, rs = sm.next()
            act(junk[:nt, 0:dim], src, AF.Square, rsrc, [rj, rs], scale=float(dim) ** -0.5, accum_out=s[:nt, 0:1])
            act(s[:nt, 1:2], s[:nt, 0:1], AF.Ln, [rs], [rs], bias=eps_t[:nt, 0:1])
            act(s[:nt, 2:3], s[:nt, 1:2], AF.Exp, [rs], [rs], scale=-0.5)
            return s[:nt, 2:3], rs

        def norm_transpose(tiles, gcol, l, group=4):
            if group < len(tiles):
                for g0 in range(0, len(tiles), group):
                    norm_transpose(tiles[g0:g0 + group], gcol, l, group)
                return
            ntmax = max(nt for (_, nt) in tiles)
            nT = len(tiles)
            sst, rss = sm.next()
            junk, rj = hb.next()
            for i_, (t, nt) in enumerate(tiles):
                act(junk[:nt, :], Xb[:nt, t, :], AF.Square, [r_Xb[t]], [rj, rss], scale=float(D) ** -0.5,
                    accum_out=sst[:nt, i_:i_ + 1])
            act(sst[:ntmax, 4:4 + nT], sst[:ntmax, 0:nT], AF.Ln, [rss], [rss], bias=eps_t[:ntmax, 0:1])
            act(sst[:ntmax, 8:8 + nT], sst[:ntmax, 4:4 + nT], AF.Exp, [rss], [rss], scale=-0.5)
            for i_, (t, nt) in enumerate(tiles):
                h, rh = hb.next()
                r, rr = sst[:nt, 8 + i_:9 + i_], rss
                act(h[:nt, :], Xb[:nt, t, :], AF.Copy, [r_Xb[t], rr], [rh], scale=r)
                pb, rpb = psA.next()
                pbb = pb[:].bitcast(BF16)
                trs([(pbb[:, kc * 128:kc * 128 + nt], h[:nt, kc * 128:(kc + 1) * 128], identb[:nt, :nt])
                     for kc in range(8)], [rh, r_const], [rpb])
                tt("dve", hT[:, :, t * 128:t * 128 + nt],
                   pbb.rearrange("p (k c) -> p k c", c=128)[:, :, 0:nt],
                   gcol[:, :].unsqueeze(2).to_broadcast([128, 8, nt]), ALU.mult, [rpb, r_lp], [r_hT[t]])

        def group_tm(l, names, widths, tiles, consumer):
            ws = [W(l, n) for n in names]
            banks = {}

            def issue(idx):
                (t, nt) = tiles[idx]
                pb, rpb = psA.next()
                items = []
                off = 0
                for (wv, rw, _), n in zip(ws, widths):
                    for kc in range(8):
                        items.append((pb[:nt, off:off + n], hT[:, kc, t * 128:t * 128 + nt], wv[:, kc, :],
                                      kc == 0, kc == 7))
                    off += n
                mm(items, [r_hT[t]] + [w_[1] for w_ in ws], [rpb])
                banks[idx] = (pb, rpb)
                if idx == len(tiles) - 1:
                    Wc(*ws)

            issue(0)
            for idx, (t, nt) in enumerate(tiles):
                if idx + 1 < len(tiles):
                    issue(idx + 1)
                pb, rpb = banks.pop(idx)
                consumer(t, nt, pb, rpb)

        def rope_tm(src, nt, nh, ptile, rsrc, out1, out2, rout, eng="pool"):
            cosb = cosT[:nt, ptile, :].unsqueeze(1).to_broadcast([nt, nh, 16])
            sinb = sinT[:nt, ptile, :].unsqueeze(1).to_broadcast([nt, nh, 16])
            ta, rta = f32a.next()
            tb, rtb = f32a.next()
            a = ta[:nt, 0:nh * 16].rearrange("p (h c) -> p h c", c=16)
            b = ta[:nt, 256:256 + nh * 16].rearrange("p (h c) -> p h c", c=16)
            c = tb[:nt, 0:nh * 16].rearrange("p (h c) -> p h c", c=16)
            d = tb[:nt, 256:256 + nh * 16].rearrange("p (h c) -> p h c", c=16)
            x1 = src[:, :, 0:16]
            x2 = src[:, :, 16:32]
            tt(eng, a, x1, cosb, ALU.mult, rsrc + [r_const], [rta])
            tt(eng, b, x2, sinb, ALU.mult, rsrc + [r_const], [rta])
            tt(eng, c, x2, cosb, ALU.mult, rsrc + [r_const], [rtb])
            tt(eng, d, x1, sinb, ALU.mult, rsrc + [r_const], [rtb])
            tt(eng, out1, a, b, ALU.subtract, [rta], rout)
            tt(eng, out2, c, d, ALU.add, [rtb], rout)

        def mla_append_tile(l, t, nt, kt, ckv_bf, r_ckv_bf, ukv_w):
            kc0 = kt * 128
            pb, rpb = psB.next()
            pbb = pb[:].bitcast(BF16)
            trs([(pbb[:, c * 128:c * 128 + nt], ckv_bf[:nt, c * 128:(c + 1) * 128], identb[:nt, :nt]) for c in range(2)],
                [r_ckv_bf, r_const], [rpb])
            cp("act", ckvT[:, :, t * 128:t * 128 + nt], pbb[:, 0:256].rearrange("p (k c) -> p k c", c=128)[:, :, 0:nt],
               [rpb], [r_ckvT[t]])
            krpad, r_krpad = krpad_b[kt % 2], r_krpad_b[kt % 2]
            pb2, rpb2 = psB.next()
            pbb2 = pb2[:].bitcast(BF16)
            trs([(pbb2[0:96, 0:nt], krpad[:nt, :], identb[:nt, :nt])], [r_krpad, r_const], [rpb2])
            cp("dve", kTm[64:96, :, kc0:kc0 + nt], pbb2[64:96, 0:nt].unsqueeze(1).to_broadcast([32, MH, nt]),
               [rpb2], [r_kTm[kt]])
            (wv, rwv, _) = ukv_w
            pb3, rpb3 = psB.next()
            mm([(pb3[:nt, :], ckvT[:, c, t * 128:t * 128 + nt], wv[:, c, :, :].rearrange("p h c -> p (h c)"),
                 c == 0, c == 1) for c in range(2)], [r_ckvT[t], rwv], [rpb3])
            cp("act", Vm[:nt, kt, :, 0:64], pb3[:nt, :].rearrange("p (h c) -> p h c", c=64), [rpb3], [r_Vm[kt]])

        def dk_append_tile(l, t, nt, kt, dk_bf, r_dk_bf):
            kc0 = kt * 128
            pb4, rpb4 = psA.next()
            pbb4 = pb4[:].bitcast(BF16)
            trs([(pbb4[:, c * 128:c * 128 + nt], dk_bf[:nt, c * 128:(c + 1) * 128], identb[:nt, :nt]) for c in range(4)],
                [r_dk_bf, r_const], [rpb4])
            cp("dve", kTd[:, :, kc0:kc0 + nt], pbb4[:, 0:512].rearrange("p (k c) -> p k c", c=128)[:, :, 0:nt],
               [rpb4], [r_kTd[kt]])

        def knope_block(l, tiles, kt0, nb, ukk_w):
            (wk, rwk, _) = ukk_w
            rts = [r_ckvT[t] for (t, _) in tiles]
            wts = [r_kTm[kt0 + t] for (t, _) in tiles]
            for h in range(MH):
                pb, rpb = psA.next()
                mm([(pb[0:64, 0:nb], wk[:, c, h, :], ckvT[:, c, 0:nb], c == 0, c == 1) for c in range(2)],
                   rts + [rwk], [rpb])
                cp("act" if h % 2 == 0 else "dve", kTm[0:64, h, kt0 * 128:kt0 * 128 + nb], pb[0:64, 0:nb], [rpb], wts)

        def attention(kind, tiles, nb, kts, diag0, brbase):
            nmap = 1 if kind == "m" else 2
            psS = Ring(psA.tiles[0:3], "psS")
            psS.res = psA.res[0:3]
            psF, rpsF = psA.tiles[3], psA.res[3]
            steps = []
            for h in range(8):
                for (kt, nk) in kts:
                    for j in range(nmap):
                        steps.append((h, kt, nk, j))
            LOOK = 2
            sc_q = {}
            pending = []
            state = {}
            kt_first, kt_last = kts[0][0], kts[-1][0]
            Vbuf, rV = (Vm_flat, r_Vm) if kind == "m" else (Vd_flat, r_Vd)
            scale = MLA_SCALE if kind == "m" else DF_SCALE

            def run_pending(upto, head_le=None):
                keep = []
                for (due, hd, fn) in pending:
                    if (upto is not None and due <= upto) or (head_le is not None and hd <= head_le) or \
                            (upto is None and head_le is None):
                        fn()
                    else:
                        keep.append((due, hd, fn))
                pending[:] = keep

            def issue(i):
                h, kt, nk, j = steps[i]
                if kind == "d" and h % 2 == 0 and kt == kt_first and j == 0:
                    c = h // 2
                    for v in range(4):
                        ts("dve", dqT1[:, v, 0:nb], dqT0[:, c, 0:nb], m01[:, v:v + 1], None, ALU.mult, None,
                           [r_dqT[c], r_const], [r_dqv])
                qlo = 0 if (diag0 is None or kt < diag0) else 128 * (kt - diag0)
                pb, rpb = psS.next()
                if kind == "m":
                    lhsT = kTm[:, h, kt * 128:kt * 128 + nk]
                    rhs = SA[:, 12 + h, qlo:nb]
                    rd = [r_kTm[kt], r_SA[12 + h], r_const]
                else:
                    lhsT = kTd[:, h // 2, kt * 128:kt * 128 + nk]
                    rhs = dqT1[:, (h % 2) * 2 + j, qlo:nb]
                    rd = [r_kTd[kt], r_dqv]
                mm([(pb[:nk, qlo:nb], lhsT, rhs, True, True)], rd, [rpb])
                sc_q[i] = (pb, rpb, qlo)

            for i in range(min(LOOK, len(steps))):
                issue(i)
            for i, (h, kt, nk, j) in enumerate(steps):
                run_pending(i)
                if i + LOOK < len(steps):
                    issue(i + LOOK)
                pb, rpb, qlo = sc_q.pop(i)
                first = (kt == kt_first)
                last = (kt == kt_last)
                if first and j == 0:
                    run_pending(None, head_le=h - 2)
                    state["ot"] = [psB.next() for _ in range(nmap)]
                pt, rpt = Pr.next()
                act(pt[:nk, qlo:nb], pb[:nk, qlo:nb], AF.Exp, [rpb], [rpt], scale=scale)
                if diag0 is not None and kt >= diag0:
                    memset("pool", pt[64:128, qlo:qlo + 64], 0.0, [rpt])
                ot, rot = state["ot"][j]
                v0 = kt * 520 + h * 65
                rdv = [rV[kt], rpt, r_const] + ([rV[kt + 1]] if (h == 7 and kt + 1 < KTMAX) else [])
                mm([(ot[:, qlo:nb], Vbuf[:nk, v0:v0 + 128], pt[:nk, qlo:nb], first, last)], rdv, [rot])
                if last and j == nmap - 1:
                    run_pending(None, head_le=h - 2)
                    base = max(i, state.get("fin_base", 0))
                    stages, busy = finish_head(kind, h, nb, state["ot"], brbase, psF, rpsF)
                    for (d_, fn) in stages:
                        pending.append((base + d_, h, fn))
                    state["fin_base"] = base + busy
            run_pending(None)

        def finish_head(kind, h, nb, ots, brbase, psF, rpsF):
            slot = brbase + h // 2
            hb_ = (h % 2) * 64
            dst = SA[hb_:hb_ + 64, slot, 0:nb]
            rows = [((h % 2) * 2 + j if kind == "d" else h % 4) for j in range(len(ots))]
            pre = []
            for j, (ot, rot) in enumerate(ots):
                jj = rows[j]

                def r_ln(j=j, ot=ot, rot=rot):
                    act(rcp[64:65, j, 0:nb], ot[64:65, 0:nb], AF.Ln, [rot], [r_rcp[j]])

                def r_exp(j=j, jj=jj):
                    act(rcp[64:65, j, 0:nb], rcp[64:65, j, 0:nb], AF.Exp, [r_rcp[j]], [r_rcp[j]], scale=-1.0)
                    if kind == "d" and j == 1:
                        ts("dve", rcp[64:65, 1, 0:nb], rcp[64:65, 1, 0:nb], lam2[64:65, 3:4], None, ALU.mult, None,
                           [r_rcp[1], r_lp], [r_rcp[1]])
                    cp("dve", rcph[64:65, jj * 2, 0:nb], rcp[64:65, j, 0:nb], [r_rcp[j]], [r_rcph[jj]])
                    tt("dve", rcph[64:65, jj * 2 + 1, 0:nb], rcp[64:65, j, 0:nb], rcph[64:65, jj * 2, 0:nb], ALU.subtract,
                       [r_rcp[j], r_rcph[jj]], [r_rcph[jj]])
                pre += [(1 + 2 * j, r_ln), (2 + 2 * j, r_exp)]
            off = 2 * len(ots)
            A, rA = f32a.next()

            def bc_mm(jj):
                mm([(psF[0:64, 0:nb], ones_b[64:65, 0:64], rcph[64:65, jj * 2, 0:nb], True, False),
                    (psF[0:64, 0:nb], ones_b[64:65, 0:64], rcph[64:65, jj * 2 + 1, 0:nb], False, True)],
                   [r_rcph[jj], r_const], [rpsF])

            if kind == "m":
                ot, rot = ots[0]

                def m2():
                    bc_mm(rows[0])

                def m4():
                    cp("dve", A[0:64, 0:nb], psF[0:64, 0:nb], [rpsF], [rA])
                    tt("dve", dst, ot[0:64, 0:nb], A[0:64, 0:nb], ALU.mult, [rot, rA], [r_SA[slot]])
                return pre + [(off + 2, m2), (off + 4, m4)], off + 5
            B, rB = f32a.next()
            (ot0, rot0), (ot1, rot1) = ots
            st_ = {}

            def d2():
                bc_mm(rows[0])

            def d4():
                cp("dve", A[0:64, 0:nb], psF[0:64, 0:nb], [rpsF], [rA])
                bc_mm(rows[1])

            def d6():
                tt("dve", A[0:64, 0:nb], ot0[0:64, 0:nb], A[0:64, 0:nb], ALU.mult, [rot0, rA], [rA])
                cp("dve", B[0:64, 0:nb], psF[0:64, 0:nb], [rpsF], [rB])
                tt("dve", B[0:64, 0:nb], ot1[0:64, 0:nb], B[0:64, 0:nb], ALU.mult, [rot1, rB], [rB])
                tt("dve", A[0:64, 0:nb], A[0:64, 0:nb], B[0:64, 0:nb], ALU.add, [rA, rB], [rA])
                st_["sq"] = b16a.next()
                sq, rsq = st_["sq"]
                tt("pool", sq[0:64, 0:nb], A[0:64, 0:nb], A[0:64, 0:nb], ALU.mult, [rA], [rsq])

            def d9():
                sq, rsq = st_["sq"]
                mm([(psF[0:64, 0:nb], ones_b[0:64, 0:64], sq[0:64, 0:nb], True, True)], [rsq, r_const], [rpsF])

            def d11():
                act(B[0:64, 0:nb], psF[0:64, 0:nb], AF.Ln, [rpsF, r_const], [rB], scale=1.0 / DDV, bias=eps_t[0:64, 0:1])
                act(B[0:64, 0:nb], B[0:64, 0:nb], AF.Exp, [rB], [rB], scale=-0.5)

            def d13():
                stt(dst, A[0:64, 0:nb], gdf[0:64, 0:1], B[0:64, 0:nb], ALU.mult, ALU.mult, [rA, rB, r_lp], [r_SA[slot]])
            return pre + [(off + 2, d2), (off + 4, d4), (off + 6, d6), (off + 8, d9), (off + 10, d11), (off + 12, d13)], off + 11

        def hgrn_tile(l, t, nt, banks, pre_next=None):
            (pq, rpq), (pf, rpf), (pi, rpi), (pg, rpg) = banks
            qh, rqh = b16a.next()
            f, rf = f32a.next()
            lf, rlf = f32a.next()
            gs, rgs = f32a.next()
            act(lf[:nt, :], pq[:nt, :], AF.Sigmoid, [rpq], [rlf])
            act(gs[:nt, :], pg[:nt, :], AF.Sigmoid, [rpg], [rgs])
            act(f[:nt, :], pf[:nt, :], AF.Sigmoid, [rpf], [rf])
            tt("dve", qh[:nt, :], pq[:nt, :], lf[:nt, :], ALU.mult, [rpq, rlf], [rqh])
            tt("dve", gs[:nt, :], pg[:nt, :], gs[:nt, :], ALU.mult, [rpg, rgs], [rgs])
            kb, rkb = b16a.next()
            stt(lf[:nt, :], f[:nt, :], -1.0, lb_bc[:nt, :], ALU.add, ALU.mult, [rf, r_lp], [rlf])
            tt("dve", f[:nt, :], f[:nt, :], lf[:nt, :], ALU.subtract, [rf, rlf], [rf])
            ts("dve", kb[:nt, :], f[:nt, :], -1.0, 1.0, ALU.mult, ALU.add, [rf], [rkb])
            ts("dve", lf[:nt, :], f[:nt, :], F_MIN, None, ALU.max, None, [rf, rlf], [rlf])
            vb, rvb = b16a.next()
            cp("dve", vb[:nt, :], pi[:nt, :], [rpi], [rvb])
            pq2, rpq2 = psA.next()
            pqb = pq2[:].bitcast(BF16)
            trs([(pqb[:, c * 128:c * 128 + nt], qh[:nt, c * 128:(c + 1) * 128], identb[:nt, :nt]) for c in range(4)],
                [rqh, r_const], [rpq2])
            pk2, rpk2 = psA.next()
            pkb = pk2[:].bitcast(BF16)
            trs([(pkb[:, c * 128:c * 128 + nt], kb[:nt, c * 128:(c + 1) * 128], identb[:nt, :nt]) for c in range(4)],
                [rkb, r_const], [rpk2])
            qT = pqb[:, 0:512].rearrange("p (k c) -> p k c", c=128)
            kT = pkb[:, 0:512].rearrange("p (k c) -> p k c", c=128)
            lfh, rlfh = b16a.next()
            lfl, rlfl = b16a.next()
            act(lf[:nt, :], lf[:nt, :], AF.Ln, [rlf], [rlf])
            cp("dve", lfh[:nt, :], lf[:nt, :], [rlf], [rlfh])
            tt("dve", lfl[:nt, :], lf[:nt, :], lfh[:nt, :], ALU.subtract, [rlf, rlfh], [rlfl])
            pe1, rpe1 = psB.next()
            mm([(pe1[:nt, :], hgm[:nt, 448:448 + nt], lfh[:nt, :], True, False),
                (pe1[:nt, :], hgm[:nt, 448:448 + nt], lfl[:nt, :], False, True)], [rlfh, rlfl, r_const], [rpe1])
            ekk, rekk = f32a.next()
            act(ekk[:nt, :], pe1[:nt, :], AF.Exp, [rpe1], [rekk])
            kk, rkk = b16a.next()
            tt("dve", kk[:nt, :], kb[:nt, :], ekk[:nt, :], ALU.mult, [rkb, rekk], [rkk])
            for c in range(4):
                pe2, rpe2 = psB.next()
                mm([(pe2[:, 0:448], lfh[:nt, c * 128:(c + 1) * 128], hgm[:nt, 0:448], True, False),
                    (pe2[:, 0:448], lfl[:nt, c * 128:(c + 1) * 128], hgm[:nt, 0:448], False, True)],
                   [rlfh, rlfl, r_const], [rpe2])
                act(EX[:, c, :], pe2[:, 0:448], AF.Exp, [rpe2], [r_EX])
            Qt, rQt = hgp.next()
            Ko, rKo = hgp.next()
            Qb, rQb = hgp.next()
            Kc, rKc = hgp.next()
            tt("dve", Qt[:, :, 0:nt], qT[:, :, 0:nt], EX[:, :, 0:nt], ALU.mult, [rpq2, r_EX], [rQt])
            tt("dve", Ko[:, :, 0:nt], kT[:, :, 0:nt], EX[:, :, 128:128 + nt], ALU.mult, [rpk2, r_EX], [rKo])
            tt("dve", Qb[:, :, 0:nt], qT[:, :, 0:nt], EX[:, :, 256:256 + nt], ALU.mult, [rpq2, r_EX], [rQb])
            cross = nt > 64
            if cross:
                tt("dve", Kc[:, :, 0:64], kT[:, :, 0:64], EX[:, :, 384:448], ALU.mult, [rpk2, r_EX], [rKc])
            pas = [psA.next(), psA.next()]
            items = []
            for h in range(8):
                c, b0, i2 = h // 2, (h % 2) * 64, h // 2
                bank = pas[h % 2][0]
                items.append((bank[:nt, i2 * 128:i2 * 128 + nt], Ko[b0:b0 + 64, c, 0:nt], Qt[b0:b0 + 64, c, 0:nt], True, True))
            mm(items, [rKo, rQt], [pas[0][1], pas[1][1]])
            for par in range(2):
                bank, rb = pas[par]
                tt("dve", att_sb[:nt, par * 4:par * 4 + 4, 0:nt],
                   bank[:nt, :].rearrange("p (h c) -> p h c", c=128)[:, :, 0:nt],
                   maskbd[:nt, 0:nt].unsqueeze(1).to_broadcast([nt, 4, nt]), ALU.mult, [rb, r_const], [r_att])
            if cross:
                pcs = [psA.next(), psA.next()]
                items = []
                for h in range(8):
                    c, b0, i2 = h // 2, (h % 2) * 64, h // 2
                    bank = pcs[h % 2][0]
                    items.append((bank[0:64, i2 * 64:(i2 + 1) * 64], Kc[b0:b0 + 64, c, 0:64], Qt[b0:b0 + 64, c, 64:128], True, True))
                mm(items, [rKc, rQt], [pcs[0][1], pcs[1][1]])
                for par in range(2):
                    bank, rb = pcs[par]
                    cp("act", att_sb[0:64, par * 4:par * 4 + 4, 64:128], bank[0:64, 0:256].rearrange("p (h c) -> p h c", c=64),
                       [rb], [r_att])
            po, rpo = psB.next()
            items = []
            for h in range(8):
                c, b0 = h // 2, (h % 2) * 64
                items.append((po[:nt, h * 64:(h + 1) * 64], att_sb[:nt, (h % 2) * 4 + h // 2, 0:nt], vb[:nt, h * 64:(h + 1) * 64], True, False))
                items.append((po[:nt, h * 64:(h + 1) * 64], Qb[b0:b0 + 64, c, 0:nt], Sbf[b0:b0 + 64, c, :], False, True))
            mm(items, [r_att, rvb, rQb, r_Sbf], [rpo])
            pd, rpd = psB.next()
            mm([(pd[:, c * 128:(c + 1) * 128], kk[:nt, c * 128:(c + 1) * 128], vb[:nt, c * 128:(c + 1) * 128], True, True)
                for c in range(4)], [rkk, rvb], [rpd])
            if pre_next is not None:
                pre_next()
            sq, rsq = f32a.next()
            act(sq[:nt, :], po[:nt, :], AF.Square, [rpo], [rsq], scale=float(HDV) ** -0.5)
            s, rs = sm.next()
            P.op("dve", lambda e: e.reduce_sum(out=s[:nt, 0:8], in_=sq[:nt, :].rearrange("p (h c) -> p h c", c=64), axis=AX.X),
                 [rsq], [rs])
            act(s[:nt, 0:8], s[:nt, 0:8], AF.Ln, [rs, r_const], [rs], bias=eps_t[:nt, 0:1])
            act(s[:nt, 8:16], s[:nt, 0:8], AF.Exp, [rs], [rs], scale=-0.5)
            tt("dve", sq[:nt, :].rearrange("p (h c) -> p h c", c=64), po[:nt, :].rearrange("p (h c) -> p h c", c=64),
               s[:nt, 8:16].unsqueeze(2).to_broadcast([nt, 8, 64]), ALU.mult, [rpo, rs], [rsq])
            ob, rob = b16a.next()
            tt("dve", ob[:nt, :], sq[:nt, :], gs[:nt, :], ALU.mult, [rsq, rgs], [rob])
            pt_, rpt_ = psA.next()
            ptb = pt_[:].bitcast(BF16)
            trs([(ptb[:, c * 128:c * 128 + nt], ob[:nt, c * 128:(c + 1) * 128], identb[:nt, :nt]) for c in range(4)],
                [rob, r_const], [rpt_])
            act(SA[:, 4:8, t * 128:t * 128 + nt], ptb[:, 0:512].rearrange("p (k c) -> p k c", c=128)[:, :, 0:nt], AF.Copy,
                [rpt_, r_lp], [r_SA[4], r_SA[5], r_SA[6], r_SA[7]], scale=ghg2[:, 0:1])
            dcol = 256 + nt - 1
            for c in range(4):
                for b0 in (0, 64):
                    P.op("dve", lambda e, c=c, b0=b0: e.scalar_tensor_tensor(
                        out=S32[b0:b0 + 64, c, :], in0=S32[b0:b0 + 64, c, :], scalar=EX[b0:b0 + 64, c, dcol:dcol + 1],
                        in1=pd[b0:b0 + 64, c * 128 + b0:c * 128 + b0 + 64], op0=ALU.mult, op1=ALU.add),
                        [r_S32, r_EX, rpd, r_Sbf], [r_S32])
            cp("act", Sbf[:], S32[:], [r_S32], [r_Sbf])

        def do_block(l, sq, j):
            is_s = sq["kind"] == "s"
            si = sq["idx"]
            if is_s:
                tiles = [(0, TS)]
                nb = TS
                kt0 = PT_S
                tok0 = 0
            else:
                tiles = [(t, 128) for t in range(4)]
                nb = 512
                kt0 = 4 * j
                tok0 = 512 * j
            last_layer = (l == DEPTH - 1)

            def load_x():
                for (t, nt) in tiles:
                    if l == 0:
                        src = x_sample[0, 0:nt, :] if is_s else x_prompt[si, tok0 + t * 128:tok0 + t * 128 + nt, :]
                        dma(Xb[:nt, t, :], src, w=[r_Xb[t]])
                    else:
                        dma(Xb[:nt, t, :], xs[si, tok0 + t * 128:tok0 + t * 128 + nt, :], r=[r_xs[si][j]], w=[r_Xb[t]])
            if is_s:
                ufence(r_Xb)
            load_x()
            norm_transpose(tiles, gmix, l, group=2)
            dbgdump("hT", hT[:, :, :], r_hT)
            stage("A")
            ufence(PH_QKV)

            def cons_q(t, nt, pb, rpb):
                h, rh = hb.next()
                r, rr = rstd(pb[:nt, 0:QL], nt, QL, [rpb], (h, rh))
                act(h[:nt, 0:QL], pb[:nt, 0:QL], AF.Copy, [rpb, rr], [rh], scale=r)
                pt_, rpt_ = psA.next()
                ptb = pt_[:].bitcast(BF16)
                trs([(ptb[:, c * 128:c * 128 + nt], h[:nt, c * 128:(c + 1) * 128], identb[:nt, :nt]) for c in range(3)],
                    [rh, r_const], [rpt_])
                tt("dve", qlatT[:, :, t * 128:t * 128 + nt], ptb[:, 0:384].rearrange("p (k c) -> p k c", c=128)[:, :, 0:nt],
                   gq[:, :].unsqueeze(2).to_broadcast([128, 3, nt]), ALU.mult, [rpt_, r_lp], [r_qlatT[t]])
            memset("pool", SA[96:128, 12:20, :], 0.0, [r_SA[12 + h] for h in range(8)])
            group_tm(l, ["q0", "q1"], [256, 128], tiles, cons_q)
            uq0 = W(l, "uq0")
            uq1 = W(l, "uq1")
            uqb = {}

            def issue_uq(idx):
                (t, nt) = tiles[idx]
                lst = []
                for half, (wv, rw, _) in enumerate((uq0, uq1)):
                    pb, rpb = psA.next()
                    mm([(pb[:nt, 0:384], qlatT[:, c, t * 128:t * 128 + nt], wv[:, c, :], c == 0, c == 2) for c in range(3)],
                       [r_qlatT[t], rw], [rpb])
                    lst.append((pb, rpb))
                uqb[idx] = lst

            issue_uq(0)
            for idx, (t, nt) in enumerate(tiles):
                qtm, r_qtm, qr, r_qr = qtm_b[t % 2], r_qtm_b[t % 2], qr_b[t % 2], r_qr_b[t % 2]
                for half, (pb, rpb) in enumerate(uqb.pop(idx)):
                    pv = pb[:nt, 0:384].rearrange("p (h c) -> p h c", c=96)
                    cp("act", qtm[:nt, half * 4:half * 4 + 4, 0:64], pv[:, :, 0:64], [rpb], [r_qtm])
                    cp("dve", qr[:nt, half * 4:half * 4 + 4, :], pv[:, :, 64:96], [rpb], [r_qr])
                if idx + 1 < len(tiles):
                    issue_uq(idx + 1)
                rope_tm(qr[:nt, :, :], nt, 8, kt0 + t, [r_qr], qtm[:nt, :, 64:80], qtm[:nt, :, 80:96], [r_qtm], eng="dve")
                pt_, rpt_ = psA.next()
                ptb = pt_[:].bitcast(BF16)
                trs([(ptb[0:96, h * 128:h * 128 + nt], qtm[:nt, h, :], identb[:nt, :nt]) for h in range(8)],
                    [r_qtm, r_const], [rpt_])
                cp("dve", SA[0:96, 12:20, t * 128:t * 128 + nt],
                   ptb[0:96, :].rearrange("p (h c) -> p h c", c=128)[:, :, 0:nt], [rpt_], [r_SA[12 + h] for h in range(8)])
            Wc(uq0, uq1)
            stage("q")

            ukv_w = W(l, "ukv")
            ukk_w = W(l, "ukk")

            def cons_kv(t, nt, pb, rpb):
                qr, r_qr = qr_b[t % 2], r_qr_b[t % 2]
                krpad, r_krpad = krpad_b[(kt0 + t) % 2], r_krpad_b[(kt0 + t) % 2]
                r, rr = rstd(pb[:nt, 0:KVL], nt, KVL, [rpb])
                sg_, rsg = stg.next()
                stt(sg_[:nt, 0:KVL], pb[:nt, 0:KVL], r, gkv_bc[:nt, :], ALU.mult, ALU.mult, [rpb, rr, r_lp], [rsg])
                cp("act", qr[:nt, 0, :], pb[:nt, KVL:KVL + MR], [rpb], [r_qr])
                if "krrope" not in SKIP:
                    rope_tm(qr[:nt, 0:1, :], nt, 1, kt0 + t, [r_qr],
                            sg_[:nt, 256:272].unsqueeze(1), sg_[:nt, 272:288].unsqueeze(1), [rsg], eng="dve")
                oc, okr = (s_ckv, s_krope) if is_s else (p_ckv, p_krope)
                bi = 0 if is_s else si
                r0 = tok0 + t * 128
                if "kvout" not in SKIP:
                    fin.append(dma(oc[l, bi, r0:r0 + nt, :], sg_[:nt, 0:KVL], r=[rsg]))
                    fin.append(dma(okr[l, bi, r0:r0 + nt, :], sg_[:nt, 256:288], r=[rsg]))
                cb, rcb = b16a.next()
                if "kvcp" not in SKIP:
                    cp("dve", cb[:nt, 0:KVL], sg_[:nt, 0:KVL], [rsg], [rcb])
                    cp("dve", krpad[:nt, 64:96], sg_[:nt, 256:288], [rsg], [r_krpad])
                if "append" not in SKIP:
                    mla_append_tile(l, t, nt, kt0 + t, cb, rcb, ukv_w)
            group_tm(l, ["kv", "kr"], [256, 32], tiles, cons_kv)
            if "knope" not in SKIP:
                knope_block(l, tiles, kt0, nb, ukk_w)
            Wc(ukv_w, ukk_w)
            stage("kv")
            for dst_, src_, r_ in cast_state["late0"]:
                dma(dst_, src_, w=[r_], q="pool")
            cast_state["late0"] = []
            ufence(PH_HG)

            hw = [W(l, n) for n in ("hq0", "hq1", "hf0", "hf1", "hi0", "hi1", "hg0", "hg1")]
            if j == 0 or is_s:
                if is_s:
                    dma(S32[:], st_hg[l, 0].rearrange("(c two) k v -> (two k) c v", two=2), w=[r_S32])
                    pass
                else:
                    memset("pool", S32[:], 0.0, [r_S32])
                cp("act", Sbf[:], S32[:], [r_S32], [r_Sbf])
            def issue_groups(idx):
                (t, nt) = tiles[idx]
                banks = []
                for g in range(4):
                    pb, rpb = (psB if g >= 2 else psA).next()
                    items = []
                    for half in range(2):
                        wv, rw, _ = hw[g * 2 + half]
                        for kc in range(8):
                            items.append((pb[:nt, half * 256:(half + 1) * 256], hT[:, kc, t * 128:t * 128 + nt], wv[:, kc, :],
                                          kc == 0, kc == 7))
                    mm(items, [r_hT[t], hw[g * 2][1], hw[g * 2 + 1][1]], [rpb])
                    if idx == len(tiles) - 1:
                        Wc(hw[g * 2], hw[g * 2 + 1])
                    banks.append((pb, rpb))
                return banks

            hstate = {"next": issue_groups(0)}
            for idx, (t, nt) in enumerate(tiles):
                banks = hstate["next"]

                def pre(idx=idx):
                    if idx + 1 < len(tiles):
                        hstate["next"] = issue_groups(idx + 1)
                hgrn_tile(l, t, nt, banks, pre)
            if (not is_s and j == sq["nblk"] - 1) or is_s:
                oh = s_hgrn if is_s else p_hgrn
                bi = 0 if is_s else si
                fin.append(dma(oh[l, bi].rearrange("(c two) k v -> (two k) c v", two=2), S32[:], r=[r_S32]))

            stage("hg")
            P.op("pool", lambda e: e.memset(fdummy[:, 0:1], 0.0), (), aext.res + r_dqT + [r_dqv, r_fd])
            for half in range(2):
                wq_ = W(l, "dq%d" % half)
                wv, rw, _ = wq_
                for m in range(2):
                    c = half * 2 + m
                    pb, rpb = psA.next()
                    mm([(pb[:, 0:nb], wv[:, kc, m * 128:(m + 1) * 128], hT[:, kc, 0:nb], kc == 0, kc == 7) for kc in range(8)],
                       [rw] + [r_hT[t] for (t, _) in tiles], [rpb])
                    cp("act", dqT0[:, c, 0:nb], pb[:, 0:nb], [rpb], [r_dqT[c]])
                Wc(wq_)
                Wc(wq_)

            def cons_dk(t, nt, pb, rpb):
                sg_, rsg = stg.next()
                cp("act", sg_[:nt, :], pb[:nt, :], [rpb], [rsg])
                od = s_dk if is_s else p_dk
                bi = 0 if is_s else si
                r0 = tok0 + t * 128
                fin.append(dma(od[l, bi, r0:r0 + nt, :], sg_[:nt, :], r=[rsg]))
                db, rdb = b16a.next()
                cp("dve", db[:nt, :], pb[:nt, :], [rpb], [rdb])
                dk_append_tile(l, t, nt, kt0 + t, db, rdb)
            group_tm(l, ["dk0", "dk1"], [256, 256], tiles, cons_dk)

            def cons_dv(t, nt, pb, rpb):
                sg_, rsg = stg.next()
                cp("act", sg_[:nt, :], pb[:nt, :], [rpb], [rsg])
                od = s_dv if is_s else p_dv
                bi = 0 if is_s else si
                r0 = tok0 + t * 128
                fin.append(dma(od[l, bi, r0:r0 + nt, :], sg_[:nt, :], r=[rsg]))
                cp("dve", Vd[:nt, kt0 + t, :, 0:64], pb[:nt, :].rearrange("p (h c) -> p h c", c=64), [rpb], [r_Vd[kt0 + t]])
            group_tm(l, ["dv0", "dv1"], [256, 256], tiles, cons_dv)

            stage("dv")
            ufence(PH_ATT)
            if is_s:
                kts = [(kt, 128) for kt in range(PT_S)] + [(PT_S, TS)]
                diag0 = None
            else:
                kts = [(kt, 128) for kt in range(kt0 + 4)]
                diag0 = kt0
            attention("m", tiles, nb, kts, diag0, 0)
            stage("attm")
            attention("d", tiles, nb, kts, diag0, 8)
            stage("attd")
            ufence(r_Xb)
            load_x()
            dbgdump("brT", SA[:, 0:12, :], r_SA[0:12])

            psA.i = 0
            for d in range(8):
                wgA, wgB, wbr = W(l, "gA%d" % d), W(l, "gB%d" % d), W(l, "br%d" % d)
                (gA, rgA, _), (gB, rgB, _), (br, rbr, _) = wgA, wgB, wbr
                acc, racc = f32a.next()
                rts = [r_hT[t] for (t, _) in tiles]
                for n in range(3):
                    pg_, rpg_ = psA.next()
                    mm([(pg_[:, 0:nb], (gA[:, kc, n, :] if n < 2 else gB[:, kc, :]), hT[:, kc, 0:nb], kc == 0, kc == 7)
                        for kc in range(8)], rts + [rgA if n < 2 else rgB], [rpg_])
                    pp, rpp = psB.next()
                    mm([(pp[:, 0:nb], br[:, kc, n, :], SA[:, n * 4 + kc, 0:nb], kc == 0, kc == 3) for kc in range(4)],
                       [rbr] + [r_SA[n * 4 + kc] for kc in range(4)], [rpp])
                    gs, rgs = f32a.next()
                    act(gs[:, 0:nb], pg_[:, 0:nb], AF.Sigmoid, [rpg_], [rgs])
                    if n == 0:
                        tt("dve", acc[:, 0:nb], gs[:, 0:nb], pp[:, 0:nb], ALU.mult, [rgs, rpp], [racc])
                    else:
                        tt("dve", gs[:, 0:nb], gs[:, 0:nb], pp[:, 0:nb], ALU.mult, [rgs, rpp], [rgs])
                        if n == 1:
                            tt("pool", acc[:, 0:nb], acc[:, 0:nb], gs[:, 0:nb], ALU.add, [racc, rgs], [racc])
                        else:
                            tt("pool", SA[:, 12 + d, 0:nb], acc[:, 0:nb], gs[:, 0:nb], ALU.add, [racc, rgs], [r_SA[12 + d]])
                Wc(wgA, wgB, wbr)
                Wc(wgA, wgB, wbr)
            wwos = [W(l, "wo%d" % i) for i in range(4)]
            for (t, nt) in tiles:
                for half in range(2):
                    pb, rpb = psA.next()
                    items = []
                    for i2 in range(2):
                        wv, rw, _ = wwos[half * 2 + i2]
                        for kc in range(8):
                            items.append((pb[:nt, i2 * 256:(i2 + 1) * 256], SA[:, 12 + kc, t * 128:t * 128 + nt], wv[:, kc, :],
                                          kc == 0, kc == 7))
                    mm(items, [wwos[half * 2][1], wwos[half * 2 + 1][1]] + [r_SA[12 + kc] for kc in range(8)], [rpb])
                    tt("dve", Xb[:nt, t, half * 512:(half + 1) * 512], Xb[:nt, t, half * 512:(half + 1) * 512], pb[:nt, 0:512],
                       ALU.add, [rpb, r_Xb[t]], [r_Xb[t]])
            Wc(*wwos)
            dbgdump("xmid", Xb[:, :, :], r_Xb)

            stage("merge")
            P.op("pool", lambda e: e.memset(fdummy[:, 0:1], 0.0), (), aext.res + r_dqT + [r_dqv, r_fd])
            def finalize_tile(t, nt):
                if last_layer:
                    r, rr = rstd(Xb[:nt, t, :], nt, D, [r_Xb[t]])
                    for hf_ in range(2):
                        tg, rtg = stg.next()
                        dma(tg[:nt, :], norm_final_g[hf_ * 512:(hf_ + 1) * 512].partition_broadcast(nt), w=[rtg])
                        stt(Xb[:nt, t, hf_ * 512:(hf_ + 1) * 512], Xb[:nt, t, hf_ * 512:(hf_ + 1) * 512], r, tg[:nt, :],
                            ALU.mult, ALU.mult, [r_Xb[t], rr, rtg], [r_Xb[t]])
                    oy = y_sample if is_s else y_prompt
                    bi = 0 if is_s else si
                    fin.append(dma(oy[bi, tok0 + t * 128:tok0 + t * 128 + nt, :], Xb[:nt, t, :], r=[r_Xb[t]]))
                else:
                    dma(xs[si, tok0 + t * 128:tok0 + t * 128 + nt, :], Xb[:nt, t, :], r=[r_Xb[t]], w=[r_xs[si][j]])

            norm_transpose(tiles, gffn, l, group=2)
            T_end = nb
            for fh in range(2):
                for ci in range(11):
                    c = fh * 11 + ci
                    wup = W(l, "up%d" % c)
                    wv, rw, _ = wup
                    rts = [r_hT[t] for (t, _) in tiles]
                    pa, rpa = psA.next()
                    mm([(pa[:, 0:nb], wv[:, kc, 0, :], hT[:, kc, 0:nb], kc == 0, kc == 7) for kc in range(8)], rts + [rw], [rpa])
                    pv_, rpv_ = psB.next()
                    mm([(pv_[:, 0:nb], wv[:, kc, 1, :], hT[:, kc, 0:nb], kc == 0, kc == 7) for kc in range(8)], rts + [rw], [rpv_])
                    Wc(wup)
                    ae, rae = aext.next()
                    if j == 0:
                        if is_s:
                            pass
                        else:
                            memset("pool", halo[:, :, c], 0.0, [r_halo[c]])
                    cp("pool", ae[:, 0:2], halo[:, :, c], [r_halo[c]], [rae])
                    cp("act", ae[:, 2:2 + nb], pa[:, 0:nb], [rpa], [rae])
                    cp("pool", halo[:, :, c], ae[:, nb:nb + 2], [rae], [r_halo[c]])
                    cv, rcv = f32a.next()
                    ts("dve", cv[:, 0:nb], ae[:, 0:nb], convw[:, c, 0:1], convb[:, c:c + 1], ALU.mult, ALU.add, [rae, r_lp], [rcv])
                    stt(cv[:, 0:nb], ae[:, 1:1 + nb], convw[:, c, 1:2], cv[:, 0:nb], ALU.mult, ALU.add, [rae, rcv, r_lp], [rcv])
                    stt(cv[:, 0:nb], ae[:, 2:2 + nb], convw[:, c, 2:3], cv[:, 0:nb], ALU.mult, ALU.add, [rae, rcv, r_lp], [rcv])
                    act(cv[:, 0:nb], cv[:, 0:nb], AF.Silu, [rcv], [rcv])
                    tt("dve", SA[:, ci, 0:nb], cv[:, 0:nb], pv_[:, 0:nb], ALU.mult, [rcv, rpv_], [r_SA[ci]])
                if fh == 0:
                    for dh in range(2):
                        accs = {}
                        for gi in range(3):
                            wdn = W(l, "dn%d_%d_%d" % (fh, dh, gi))
                            wv, rw, _ = wdn
                            ncn = 4 if gi < 2 else 3
                            for (t, nt) in tiles:
                                if gi == 0:
                                    accs[t] = psB.next()
                                pb, rpb = accs[t]
                                mm([(pb[:nt, :], SA[:, gi * 4 + cc, t * 128:t * 128 + nt], wv[:, cc, :],
                                     gi == 0 and cc == 0, gi == 2 and cc == ncn - 1) for cc in range(ncn)],
                                   [rw] + [r_SA[gi * 4 + cc] for cc in range(ncn)], [rpb])
                                if gi == 2:
                                    tt("dve", Xb[:nt, t, dh * 512:(dh + 1) * 512], Xb[:nt, t, dh * 512:(dh + 1) * 512], pb[:nt, :],
                                       ALU.add, [rpb, r_Xb[t]], [r_Xb[t]])
                            Wc(wdn)
                else:
                    wdns = {}
                    for dh in range(2):
                        for gi in range(3):
                            wdns[(dh, gi)] = W(l, "dn%d_%d_%d" % (fh, dh, gi))
                    for (t, nt) in tiles:
                        for dh in range(2):
                            pb, rpb = psB.next()
                            items = []
                            rws = []
                            for gi in range(3):
                                wv, rw, _ = wdns[(dh, gi)]
                                rws.append(rw)
                                ncn = 4 if gi < 2 else 3
                                for cc in range(ncn):
                                    items.append((pb[:nt, :], SA[:, gi * 4 + cc, t * 128:t * 128 + nt], wv[:, cc, :],
                                                  gi == 0 and cc == 0, gi == 2 and cc == ncn - 1))
                            mm(items, rws + [r_SA[i_] for i_ in range(11)], [rpb])
                            tt("dve", Xb[:nt, t, dh * 512:(dh + 1) * 512], Xb[:nt, t, dh * 512:(dh + 1) * 512], pb[:nt, :],
                               ALU.add, [rpb, r_Xb[t]], [r_Xb[t]])
                        finalize_tile(t, nt)
                    Wc(*wdns.values())
            if is_s or j == sq["nblk"] - 1:
                pb, rpb = psA.next()
                mm([(pb[0:2 * NFC, 0:128], halo[:, :, :].rearrange("p j c -> p (j c)"), identf[:, :], True, True)],
                   r_halo + [r_const], [rpb])
                sg_, rsg = stg.next()
                cp("act", sg_[0:2 * NFC, 0:128], pb[0:2 * NFC, 0:128], [rpb], [rsg])
                ocv = s_conv if is_s else p_conv
                bi = 0 if is_s else si
                for jj in range(2):
                    fin.append(dma(ocv[l, bi, jj].rearrange("(c p) -> c p", p=128), sg_[jj * NFC:(jj + 1) * NFC, 0:128], r=[rsg]))
            stage("ffn")

        def sample_past(l):
            ufence(PH_QKV)
            for pbk in range(PT_S // 4):
                ukv_w = W(l, "ukv")
                ukk_w = W(l, "ukk")
                tl = [(t, 128) for t in range(4)]
                for (t, nt) in tl:
                    kt = pbk * 4 + t
                    r0 = kt * 128
                    cb, rcb = b16a.next()
                    dma(cb[:, 0:KVL], c_ckv[l, 0, r0:r0 + 128, :], w=[rcb], q="pool")
                    dma(krpad_b[kt % 2][:, 64:96], c_kr[l, 0, r0:r0 + 128, :], w=[r_krpad_b[kt % 2]], q="pool")
                    db, rdb = b16a.next()
                    dma(db[:, :], c_dk[l, 0, r0:r0 + 128, :], w=[rdb], q="pool")
                    dma(Vd[:, kt, :, 0:64], c_dv[l, 0, r0:r0 + 128, :].rearrange("p (h c) -> p h c", c=64), w=[r_Vd[kt]], q="pool")
                    mla_append_tile(l, t, nt, kt, cb, rcb, ukv_w)
                    dk_append_tile(l, t, nt, kt, db, rdb)
                knope_block(l, tl, pbk * 4, 512, ukk_w)
                Wc(ukv_w, ukk_w)
            sg_, rsg = stg.next()
            for jj in range(2):
                dma(sg_[jj * NFC:(jj + 1) * NFC, 0:128], st_cv[l, 0, jj].rearrange("(c p) -> c p", p=128), w=[rsg])
            pb, rpb = psA.next()
            mm([(pb[:, 0:2 * NFC], sg_[0:2 * NFC, 0:128], identf[0:2 * NFC, 0:2 * NFC], True, True)], [rsg, r_const], [rpb])
            cp("act", halo[:, :, :].rearrange("p j c -> p (j c)"), pb[:, 0:2 * NFC], [rpb], r_halo)

        try:
            stage("setup")
            for l in range(DEPTH):
                load_layer_params(l)
                stage("params")
                nxt = conv_by_layer.get(l + 1, [])
                nblocks_p = max(1, NP * NBLK)
                cast_state["queue"] = list(nxt)
                cast_state["per_block"] = -(-len(nxt) // max(1, nblocks_p // 2))
                for sq in seqs:
                    if sq["kind"] == "s":
                        sample_past(l)
                    for j in range(sq["nblk"]):
                        do_block(l, sq, j)
                        cast_some()
                while cast_state["queue"]:
                    cast_some()
            assert wstate["used"] == len(stream)
        except StopBuild:
            print("build stopped at", STOP)
        print("sbuf bytes remaining:", nc.sbuf_bytes_remaining, "ops:", {e: len(P.ops[e]) for e in ENGS})
        P.emit(fin)
    return nc


_OUT_ORDER = ["y_prompt", "y_sample", "p_ckv", "p_krope", "p_dk", "p_dv", "p_hgrn", "p_conv",
              "s_ckv", "s_krope", "s_dk", "s_dv", "s_hgrn", "s_conv"]


def kernel(**inputs):
    return run(inputs, 8)


def run(inputs, NCORE, dbg=(), sample=True):
    inp = {k: np.asarray(v) for k, v in inputs.items()}
    B, T = inp["x_prompt"].shape[0], inp["x_prompt"].shape[1]
    SB, TS = inp["x_sample"].shape[0], inp["x_sample"].shape[1]
    DEPTH = inp["w_in"].shape[0]
    PAST = inp["cache_mla_ckv"].shape[2]
    NP = B // NCORE
    assert SB == NCORE
    nc = build(NP=NP, T=T, DEPTH=DEPTH, PAST=PAST, TS=TS, SAMPLE=sample, dbg=dbg)
    consts = _consts(max(T // 128, PAST // 128 + 1) if sample else T // 128)
    weights = ["norm_mix_g", "w_in", "mla_q_norm_g", "mla_w_uq", "mla_kv_norm_g", "mla_w_ukv", "hgrn_lb_logits",
               "hgrn_norm_g", "diff_lambda", "diff_norm_g", "w_branch", "w_out", "norm_ffn_g", "ffn_w_up",
               "ffn_conv_w", "ffn_conv_b", "ffn_w_down", "norm_final_g"]
    in_maps = []
    for c in range(NCORE):
        m = {w: np.ascontiguousarray(inp[w], dtype=np.float32) for w in weights}
        m.update(consts)
        m["x_prompt"] = np.ascontiguousarray(inp["x_prompt"][c * NP:(c + 1) * NP])
        m["x_sample"] = np.ascontiguousarray(inp["x_sample"][c:c + 1])
        m["cache_mla_ckv"] = np.ascontiguousarray(inp["cache_mla_ckv"][:, c:c + 1])
        m["cache_mla_krope"] = np.ascontiguousarray(inp["cache_mla_krope"][:, c:c + 1])
        m["cache_diff_k"] = np.ascontiguousarray(inp["cache_diff_k"][:, c:c + 1]).reshape(DEPTH, 1, PAST, 512)
        m["cache_diff_v"] = np.ascontiguousarray(inp["cache_diff_v"][:, c:c + 1]).reshape(DEPTH, 1, PAST, 512)
        m["state_hgrn"] = np.ascontiguousarray(inp["state_hgrn"][:, c:c + 1])
        m["state_ffn_conv"] = np.ascontiguousarray(inp["state_ffn_conv"][:, c:c + 1])
        in_maps.append(m)
    if not sample:
        for m in in_maps:
            for k in ("x_sample", "cache_mla_ckv", "cache_mla_krope", "cache_diff_k", "cache_diff_v", "state_hgrn", "state_ffn_conv"):
                m.pop(k)
    import os
    if os.environ.get("KTRACE"):
        res = run_bass_kernel_spmd(nc, in_maps, core_ids=list(range(NCORE)), trace=True)
        print("EXEC_TIME_NS", res.exec_time_ns)
    else:
        res = run_bass_kernel_spmd(nc, in_maps, core_ids=list(range(NCORE)))
    rs = res.results
    if dbg:
        return rs
    outs = []
    for name in _OUT_ORDER:
        ax = 0 if name in ("y_prompt", "y_sample") else 1
        a = np.concatenate([np.asarray(r[name]) for r in rs], axis=ax).astype(np.float32)
        if name in ("p_dk", "p_dv", "s_dk", "s_dv"):
            a = a.reshape(a.shape[0], a.shape[1], a.shape[2], 8, 64)
        outs.append(a)
    return tuple(outs)
```
